# Optimizing a Trainium2 kernel written in Bass

```python
import math
import jax, jax.numpy as jnp
from jax import lax
import numpy as np

D_MODEL = 1024
BATCH = 8
SEQ = 4096
DEPTH = 2

CHUNK = 64
EPS = 1e-6
RET_HEADS = 4
RET_QK_DIM = 256
RET_V_DIM = 512
RET_Q_COLS = RET_HEADS * RET_QK_DIM
RET_V_COLS = RET_HEADS * RET_V_DIM
RET_IN_COLS = 2 * RET_Q_COLS + 2 * RET_V_COLS
ROPE_BASE = 10000.0
ATT_HEADS = 16
ATT_HEAD_DIM = D_MODEL // ATT_HEADS
PAST_CHUNKS = 8
BAND = (PAST_CHUNKS + 1) * CHUNK
REL_CLIP = 256
REL_TABLE = 2 * REL_CLIP + 1
FFN_HIDDEN = int(math.ceil(math.ceil(8 * D_MODEL / 3) / 256) * 256)

kernel_name = "yoco_retention_chunked_relbias_attention"


def rmsnorm(x, g):
    x32 = x.astype(jnp.float32)
    y = x32 * lax.rsqrt(jnp.mean(x32 * x32, axis=-1, keepdims=True) + EPS)
    return (y * g.astype(jnp.float32)).astype(x.dtype)


def swiglu_ffn(h, w_gu, w_down):
    gate, up = jnp.split(h @ w_gu, 2, axis=-1)
    return (jax.nn.silu(gate) * up) @ w_down


def rope(x, positions):
    half = x.shape[-1] // 2
    inv_freq = ROPE_BASE ** (-jnp.arange(half, dtype=jnp.float32) / half)
    ang = positions.astype(jnp.float32)[:, None] * inv_freq[None, :]
    cos = jnp.cos(ang)[None, :, None, :]
    sin = jnp.sin(ang)[None, :, None, :]
    x1, x2 = x[..., :half], x[..., half:]
    return jnp.concatenate([x1 * cos - x2 * sin, x1 * sin + x2 * cos], axis=-1).astype(x.dtype)


def retention_mixer(h, w_in, gn_g, w_o):
    b, s, _ = h.shape
    n = s // CHUNK
    proj = h @ w_in
    q, k, v, g = jnp.split(proj, [RET_Q_COLS, 2 * RET_Q_COLS, 2 * RET_Q_COLS + RET_V_COLS], axis=-1)
    pos = jnp.arange(s)
    q = rope(q.reshape(b, s, RET_HEADS, RET_QK_DIM), pos)
    k = rope(k.reshape(b, s, RET_HEADS, RET_QK_DIM), pos) * (RET_QK_DIM ** -0.5)
    v = v.reshape(b, s, RET_HEADS, RET_V_DIM)
    to_chunks = lambda t: t.reshape(b, n, CHUNK, RET_HEADS, t.shape[-1]).transpose(1, 0, 3, 2, 4)
    qc, kc, vc = to_chunks(q), to_chunks(k), to_chunks(v)

    log_gamma = jnp.log(1.0 - 2.0 ** (-5.0 - jnp.arange(RET_HEADS, dtype=jnp.float32)))
    lg = log_gamma[:, None]
    t = jnp.arange(CHUNK, dtype=jnp.float32)
    intra = jnp.exp(lg[:, :, None] * jnp.abs(t[:, None] - t[None, :]))
    q_dec = jnp.exp(lg * (t + 1.0))[:, :, None]
    k_dec = jnp.exp(lg * (CHUNK - 1.0 - t))[:, :, None]
    s_dec = jnp.exp(lg * CHUNK)[:, :, None]

    def step(state, inp):
        qi, ki, vi = inp
        scores = jnp.einsum('bhtd,bhsd->bhts', qi, ki) * intra
        o = jnp.einsum('bhts,bhsv->bhtv', scores, vi) + jnp.einsum('bhtd,bhdv->bhtv', qi * q_dec, state)
        state = state * s_dec + jnp.einsum('bhsd,bhsv->bhdv', ki * k_dec, vi)
        return state, o

    state0 = jnp.zeros((b, RET_HEADS, RET_QK_DIM, RET_V_DIM), dtype=jnp.result_type(qc.dtype, intra.dtype))
    _, o = lax.scan(step, state0, (qc, kc, vc))
    o = o.transpose(1, 0, 3, 2, 4).reshape(b, s, RET_HEADS, RET_V_DIM)
    o = rmsnorm(o, gn_g.reshape(RET_HEADS, RET_V_DIM)).reshape(b, s, RET_V_COLS).astype(h.dtype)
    return (jax.nn.silu(g) * o) @ w_o


def shared_kv(h, kv_norm_g, w_kv, k_norm_g):
    b, s, _ = h.shape
    u = rmsnorm(h, kv_norm_g)
    k, v = jnp.split(u @ w_kv, 2, axis=-1)
    k = rmsnorm(k.reshape(b, s, ATT_HEADS, ATT_HEAD_DIM), k_norm_g)
    v = v.reshape(b, s, ATT_HEADS, ATT_HEAD_DIM)
    pad = ((0, 0), (PAST_CHUNKS * CHUNK, 0), (0, 0), (0, 0))
    return jnp.pad(k, pad), jnp.pad(v, pad)


def chunk_band_attention(h, w_q, q_norm_g, rel_bias, w_o, k_pad, v_pad):
    b, s, _ = h.shape
    n = s // CHUNK
    q = rmsnorm((h @ w_q).reshape(b, s, ATT_HEADS, ATT_HEAD_DIM), q_norm_g) * (ATT_HEAD_DIM ** -0.5)
    qc = q.reshape(b, n, CHUNK, ATT_HEADS, ATT_HEAD_DIM).swapaxes(0, 1)
    t = jnp.arange(CHUNK)
    j = jnp.arange(BAND)
    dist = PAST_CHUNKS * CHUNK + t[:, None] - j[None, :]
    bias = rel_bias.astype(jnp.float32)[:, jnp.clip(dist, -REL_CLIP, REL_CLIP) + REL_CLIP]

    def one_chunk(args):
        i, qi = args
        kb = lax.dynamic_slice_in_dim(k_pad, i * CHUNK, BAND, axis=1)
        vb = lax.dynamic_slice_in_dim(v_pad, i * CHUNK, BAND, axis=1)
        sc = jnp.einsum('bthd,bshd->bhts', qi, kb).astype(jnp.float32) + bias
        valid = j >= (PAST_CHUNKS - i) * CHUNK
        sc = jnp.where(valid[None, None, None, :], sc, -jnp.inf)
        p = jax.nn.softmax(sc, axis=-1).astype(vb.dtype)
        return jnp.einsum('bhts,bshd->bthd', p, vb)

    o = lax.map(one_chunk, (jnp.arange(n), qc))
    o = o.swapaxes(0, 1).reshape(b, s, ATT_HEADS * ATT_HEAD_DIM)
    return o @ w_o


def setup_inputs(seed: int = 0) -> dict:
    key = jax.random.key(seed)
    ks = iter(jax.random.split(key, 32))
    n_a = DEPTH // 2
    n_b = DEPTH - n_a
    f32 = jnp.float32

    def w(shape, fan_in):
        return jax.random.normal(next(ks), shape, f32) * (fan_in ** -0.5)

    def gain(shape):
        return 1.0 + 0.1 * jax.random.normal(next(ks), shape, f32)

    return {
        "x": jax.random.normal(next(ks), (BATCH, SEQ, D_MODEL), f32),
        "a_norm_g": gain((n_a, D_MODEL)),
        "a_w_in": w((n_a, D_MODEL, RET_IN_COLS), D_MODEL),
        "a_gn_g": gain((n_a, RET_V_COLS)),
        "a_w_o": w((n_a, RET_V_COLS, D_MODEL), RET_V_COLS),
        "a_ffn_norm_g": gain((n_a, D_MODEL)),
        "a_w_gu": w((n_a, D_MODEL, 2 * FFN_HIDDEN), D_MODEL),
        "a_w_down": w((n_a, FFN_HIDDEN, D_MODEL), FFN_HIDDEN),
        "kv_norm_g": gain((D_MODEL,)),
        "w_kv": w((D_MODEL, 2 * D_MODEL), D_MODEL),
        "k_norm_g": gain((ATT_HEAD_DIM,)),
        "b_norm_g": gain((n_b, D_MODEL)),
        "b_w_q": w((n_b, D_MODEL, D_MODEL), D_MODEL),
        "b_q_norm_g": gain((n_b, ATT_HEAD_DIM)),
        "b_rel_bias": 0.5 * jax.random.normal(next(ks), (n_b, ATT_HEADS, REL_TABLE), f32),
        "b_w_o": w((n_b, D_MODEL, D_MODEL), D_MODEL),
        "b_ffn_norm_g": gain((n_b, D_MODEL)),
        "b_w_gu": w((n_b, D_MODEL, 2 * FFN_HIDDEN), D_MODEL),
        "b_w_down": w((n_b, FFN_HIDDEN, D_MODEL), FFN_HIDDEN),
    }


def reference(x, a_norm_g, a_w_in, a_gn_g, a_w_o, a_ffn_norm_g, a_w_gu, a_w_down,
              kv_norm_g, w_kv, k_norm_g,
              b_norm_g, b_w_q, b_q_norm_g, b_rel_bias, b_w_o, b_ffn_norm_g, b_w_gu, b_w_down):
    n_a = DEPTH // 2
    k_pad = v_pad = None
    for layer in range(DEPTH):
        if layer < n_a:
            i = layer
            x = x + retention_mixer(rmsnorm(x, a_norm_g[i]), a_w_in[i], a_gn_g[i], a_w_o[i])
            x = x + swiglu_ffn(rmsnorm(x, a_ffn_norm_g[i]), a_w_gu[i], a_w_down[i])
        else:
            if layer == n_a:
                k_pad, v_pad = shared_kv(x, kv_norm_g, w_kv, k_norm_g)
            i = layer - n_a
            x = x + chunk_band_attention(rmsnorm(x, b_norm_g[i]), b_w_q[i], b_q_norm_g[i],
                                         b_rel_bias[i], b_w_o[i], k_pad, v_pad)
            x = x + swiglu_ffn(rmsnorm(x, b_ffn_norm_g[i]), b_w_gu[i], b_w_down[i])
    return x
```

```python
import math
from contextlib import ExitStack
import numpy as np
import ml_dtypes
import concourse.bass as bass
import concourse.mybir as mybir
from concourse.bass_utils import run_bass_kernel_spmd

F32 = mybir.dt.float32
BF16 = mybir.dt.bfloat16
ALU = mybir.AluOpType
AF = mybir.ActivationFunctionType
AX = mybir.AxisListType

D = 1024
SEQ = 4096
NCORES = 8
EPS = 1e-6
FH = 2816
RH, RDK, RDV = 4, 256, 512
AH, ADH = 16, 64


class Buf:
    __slots__ = ("name", "writers", "readers")

    def __init__(self, name):
        self.name = name
        self.writers = {}
        self.readers = {}


class Chan:
    def __init__(self, name):
        self.name = name
        self.sem = None
        self.sw = False
        self.slot = None
        self.count = 0
        self.last = None


class Op:
    __slots__ = ("eng", "fn", "deps", "sig", "val", "chan", "idx", "key")

    def __init__(self, eng, fn, chan=None):
        self.eng = eng
        self.fn = fn
        self.deps = []
        self.sig = chan is not None
        self.val = None
        self.chan = chan
        self.key = chan if chan is not None else eng


COMPUTE = ("pe", "act", "dve", "pool")
QUEUES = ("pe", "act", "dve", "pool", "sp")


class Sched:
    def __init__(self, nc):
        self.nc = nc
        self.sems = {q: nc.alloc_semaphore(name="sem_" + q) for q in COMPUTE}
        self.semcount = {q: 0 for q in COMPUTE}
        self.dpool = []
        self.swpool = []
        self.nops = 0
        self.last_sw = None
        self.begin()

    def begin(self):
        self.ops = {q: [] for q in QUEUES}
        self.chans = []

    def chan(self, name, sw=False):
        c = Chan(name)
        pool = self.swpool if sw else self.dpool
        i = sum(1 for o in self.chans if o.sw == sw)
        if i >= len(pool):
            pool.append([self.nc.alloc_semaphore(name=("swsem%d" if sw else "dsem%d") % i), 0])
        c.sw = sw
        c.slot = pool[i]
        c.sem = c.slot[0]
        c.count = c.slot[1]
        self.chans.append(c)
        return c

    def _add(self, op, reads, writes):
        eng = op.eng
        key = op.key
        is_dma = op.chan is not None
        deps = {}

        def dep(o):
            k = o.key
            if k not in deps or deps[k].idx < o.idx:
                deps[k] = o

        for b in reads:
            for k, w in b.writers.items():
                if k == key and eng == "pe" and not is_dma:
                    continue
                dep(w)
        for b in writes:
            for k, w in b.writers.items():
                if k == key and not is_dma:
                    continue
                dep(w)
            for k, r in b.readers.items():
                if k == key and not is_dma:
                    continue
                dep(r)
        if is_dma and op.chan.last is not None:
            dep(op.chan.last)
        if is_dma and op.chan.sw:
            if self.last_sw is not None:
                dep(self.last_sw)
            self.last_sw = op
        op.idx = self.nops
        self.nops += 1
        op.deps = list(deps.values())
        for o in op.deps:
            o.sig = True
        for b in reads:
            b.readers[key] = op
        for b in writes:
            b.writers = {key: op}
            b.readers = {}
        if is_dma:
            op.chan.last = op
        self.ops[eng].append(op)
        return op

    def op(self, eng, fn, reads=(), writes=()):
        return self._add(Op(eng, fn), reads, writes)

    def dma(self, queue, out, in_, chan, reads=(), writes=()):
        def fn(e):
            return e.dma_start(out=out, in_=in_)

        return self._add(Op(queue, fn, chan), reads, writes)

    def emit(self):
        nc = self.nc
        sems = self.sems
        for q in QUEUES:
            for o in self.ops[q]:
                if o.chan is not None:
                    o.chan.count += 16
                    o.val = o.chan.count
                elif o.sig:
                    self.semcount[q] += 1
                    o.val = self.semcount[q]
        for c in self.chans:
            c.slot[1] = c.count
        engines = {"pe": "tensor", "act": "scalar", "dve": "vector", "pool": "gpsimd", "sp": "sync"}
        stats = {}

        def run_queue(q):
            def body(e):
                waited = {}
                nw = 0
                for o in self.ops[q]:
                    for d in o.deps:
                        s = d.chan.sem if d.chan is not None else sems[d.eng]
                        if waited.get(s, 0) >= d.val:
                            continue
                        e.wait_ge(s, d.val)
                        nw += 1
                        waited[s] = d.val
                    ins = o.fn(e)
                    if o.chan is not None:
                        ins.then_inc(o.chan.sem, 16)
                    elif o.sig:
                        ins.then_inc(sems[q], 1)
                if q == "sp":
                    for c in self.chans:
                        if c.last is not None and waited.get(c.sem, 0) < c.count:
                            e.wait_ge(c.sem, c.count)
                    for cq in COMPUTE:
                        if self.semcount[cq] > 0:
                            e.wait_ge(sems[cq], self.semcount[cq])
                stats[q] = (len(self.ops[q]), nw)

            return body

        with nc.Block() as block:
            for q in QUEUES:
                getattr(block, engines[q])(run_queue(q))
        self.stats = stats
        self.begin()


class T:
    def __init__(self, k, name, shape, dtype, psum=False, carry=False):
        nc = k.nc
        self.k = k
        if psum:
            self.h = k.es.enter_context(nc.psum_tensor("p%d_%s" % (k.phase, name), shape, dtype))
        else:
            n = 1
            for d in shape[1:]:
                n *= d
            self.nbytes = n * (2 if dtype == BF16 else 4)
            self.off = k._alloc(self.nbytes, top=False)
            self.h = nc.alloc_sbuf_tensor_at("s%d_%s" % (k.phase, name), shape, dtype, offset=self.off)
            if not carry:
                k.local.append(self)
        self.b = Buf(name)
        self.shape = shape

    def __getitem__(self, idx):
        return self.h[idx]


SB_LO, SB_HI = 16512, 229376


class K:
    def __init__(self, nc):
        self.nc = nc
        self.s = Sched(nc)
        self.es = None
        self.phase = 0
        self.free = [[SB_LO, SB_HI]]
        self.local = []

    def _alloc(self, n, top=False):
        n = (n + 63) // 64 * 64
        blocks = reversed(self.free) if top else self.free
        for blk in blocks:
            if blk[1] - blk[0] >= n:
                if top:
                    blk[1] -= n
                    off = blk[1]
                else:
                    off = blk[0]
                    blk[0] += n
                self.free = [b for b in self.free if b[1] > b[0]]
                return off
        raise MemoryError("SBUF exhausted: need %d, free %s" % (n, self.free))

    def _release(self, off, n):
        n = (n + 63) // 64 * 64
        self.free.append([off, off + n])
        self.free.sort()
        merged = []
        for b in self.free:
            if merged and merged[-1][1] == b[0]:
                merged[-1][1] = b[1]
            else:
                merged.append(list(b))
        self.free = merged

    def release(self, t):
        self._release(t.off, t.nbytes)

    def begin(self):
        self.es = ExitStack()
        self.es.__enter__()
        self.phase += 1

    def end(self):
        for q in ("act", "dve", "pool"):
            if not self.s.ops[q]:
                d = self.sb("dummy_" + q, [128, 16], F32)
                self.memset(q if q != "act" else "dve", d[:], 0.0, (d.b,))
        self.s.emit()
        self.es.__exit__(None, None, None)
        self.es = None
        for t in self.local:
            self._release(t.off, t.nbytes)
        self.local = []

    def sb(self, name, shape, dtype, carry=False):
        return T(self, name, shape, dtype, carry=carry)

    def ps(self, name, shape, dtype=F32):
        return T(self, name, shape, dtype, psum=True)

    def mm(self, out, lhsT, rhs, start, stop, reads, writes):
        self.s.op("pe", lambda e: e.matmul(out, lhsT, rhs, start=start, stop=stop), reads, writes)

    def tr(self, out, in_, ident, reads, writes):
        self.s.op("pe", lambda e: e.transpose(out, in_, ident), reads, writes)

    def act(self, out, in_, func, reads, writes, bias=None, scale=None, accum_out=None):
        kw = {}
        if bias is not None:
            kw["bias"] = bias
        if scale is not None:
            kw["scale"] = scale
        if accum_out is not None:
            kw["accum_out"] = accum_out
        self.s.op("act", lambda e: e.activation(out, in_, func, **kw), reads, writes)

    def tt(self, eng, out, in0, in1, op, reads, writes):
        self.s.op(eng, lambda e: e.tensor_tensor(out, in0, in1, op), reads, writes)

    def ts(self, eng, out, in0, s1, s2, op0, op1, reads, writes):
        if op1 is None:
            self.s.op(eng, lambda e: e.tensor_scalar(out, in0, s1, None, op0), reads, writes)
        else:
            self.s.op(eng, lambda e: e.tensor_scalar(out, in0, s1, s2, op0, op1), reads, writes)

    def stt(self, eng, out, in0, scalar, in1, op0, op1, reads, writes):
        self.s.op(eng, lambda e: e.scalar_tensor_tensor(out, in0, scalar, in1, op0, op1), reads, writes)

    def recip(self, out, in_, reads, writes):
        self.s.op("dve", lambda e: e.reciprocal(out, in_), reads, writes)

    def copy(self, eng, out, in_, reads, writes):
        if eng == "act":
            self.s.op("act", lambda e: e.copy(out, in_), reads, writes)
        elif eng == "dve":
            self.s.op(eng, lambda e: e.tensor_scalar(out, in_, 1.0, None, ALU.mult), reads, writes)
        else:
            self.s.op(eng, lambda e: e.tensor_copy(out, in_), reads, writes)

    def memset(self, eng, ap, val, writes):
        self.s.op(eng, lambda e: e.memset(ap, val), (), writes)

    def dma(self, out, in_, chan, reads, writes, queue="sp"):
        self.s.dma(queue, out, in_, chan, reads, writes)


class DT:
    def __init__(self, ap, name):
        self.ap = ap
        self.name = name
        self.bufs = {}

    def b(self, i):
        if i not in self.bufs:
            self.bufs[i] = Buf("%s_%d" % (self.name, i))
        return self.bufs[i]


def load_consts(k, ones_d, identf_d, identb_d):
    c = {}
    c["ones"] = k.sb("ones", [128, 128], BF16)
    c["identf"] = k.sb("identf", [128, 128], F32)
    c["identb"] = k.sb("identb", [128, 128], BF16)
    ch = k.s.chan("const")
    k.dma(c["ones"][:], ones_d, ch, (), (c["ones"].b,))
    k.dma(c["identf"][:], identf_d, ch, (), (c["identf"].b,))
    k.dma(c["identb"][:], identb_d, ch, (), (c["identb"].b,))
    return c


class W:
    def __init__(self, k, name, w_d, kchunks, ncols, groups, carry=False, nch=1):
        self.t = k.sb(name, [128, kchunks, ncols], BF16, carry=carry)
        self.groups = []
        chans = [k.s.chan("%s_%d" % (name, i), sw=True) for i in range(nch)]
        wv = w_d.rearrange("(c p) n -> p c n", p=128)
        for i, (c0, cn) in enumerate(groups):
            b = Buf("%s_g%d" % (name, i))
            self.groups.append((c0, c0 + cn, b))
            k.dma(self.t[:, :, c0:c0 + cn], wv[:, :, c0:c0 + cn], chans[i % nch], (), (b,), queue="pool")

    def __getitem__(self, idx):
        return self.t[idx]

    def tok(self, col):
        for c0, c1, b in self.groups:
            if c0 <= col < c1:
                return b
        raise KeyError(col)


class WStaged:
    def __init__(self, k, name, w_d, kchunks, ncols):
        self.t = k.sb(name, [128, kchunks, ncols], BF16)
        stg = [k.sb("%s_stg%d" % (name, i), [128, ncols], F32) for i in range(2)]
        chs = [k.s.chan("%s_stg%d" % (name, i)) for i in range(2)]
        self.b = Buf(name + "_all")
        self.toks = []
        for c in range(kchunks):
            s_ = stg[c % 2]
            b = Buf("%s_c%d" % (name, c))
            self.toks.append(b)
            k.dma(s_[:], w_d[c * 128:(c + 1) * 128, :], chs[c % 2], (), (s_.b,))
            k.s.op("pool", lambda e, o=self.t[:, c, :], i=s_[:]: e.tensor_copy(o, i), (s_.b,), (b,))

    def __getitem__(self, idx):
        return self.t[idx]

    def tok(self, col):
        return self.toks[-1]


def ffn_weights(k, pfx, w_gu_d, w_down_d, carry=False, hw_down=False, only=None):
    KC, HC = D // 128, FH // 128
    gu_groups = []
    c0 = 0
    for n in (256, 512, 1024, 1024):
        gu_groups.append((c0, n))
        gu_groups.append((FH + c0, n))
        c0 += n
    if only == "gu":
        return W(k, pfx + "wgu", w_gu_d, KC, 2 * FH, gu_groups, carry=carry)
    if only == "down":
        return W(k, pfx + "wd", w_down_d, HC, D, [(0, 512), (512, 512)], carry=carry)
    wgu = W(k, pfx + "wgu", w_gu_d, KC, 2 * FH, gu_groups, carry=carry)
    if hw_down:
        wd = WStaged(k, pfx + "wd", w_down_d, HC, D)
    else:
        wd = W(k, pfx + "wd", w_down_d, HC, D, [(0, 512), (512, 512)], carry=carry)
    return wgu, wd


def phase_ffn(k, cst, S, TT, xin, xout, w_gu_d, w_down_d, g_d, pfx, out_tm=None, weights=None, prefetch=None,
              hw_down=False):
    nc = k.nc
    NT = S // TT
    KC = D // 128
    HC = FH // 128
    if weights is None:
        wgu, wd = ffn_weights(k, pfx, w_gu_d, w_down_d, hw_down=hw_down)
    elif isinstance(weights, tuple):
        wgu, wd = weights
    else:
        wgu = weights
        wd = ffn_weights(k, pfx, w_gu_d, w_down_d, only="down")
    if prefetch is not None:
        prefetch()
    g = k.sb(pfx + "g", [128, KC], F32)
    chg = k.s.chan(pfx + "g")
    k.dma(g[:], g_d, chg, (), (g.b,))

    xs = [k.sb(pfx + "x%d" % i, [128, KC, TT], F32) for i in range(2)]
    xch = [k.s.chan(pfx + "xl%d" % i) for i in range(2)]
    sch = [k.s.chan(pfx + "xs%d" % i) for i in range(2)]
    sq = k.sb(pfx + "sq", [128, KC, TT], BF16)
    h2 = [k.sb(pfx + "h%d" % i, [128, KC, TT], BF16) for i in range(2)]
    a = [k.sb(pfx + "a%d" % i, [128, TT], BF16) for i in range(HC)]
    sg = [k.sb(pfx + "sg%d" % i, [128, TT], F32) for i in range(2)]
    rstd = k.sb(pfx + "rstd", [128, TT], F32)
    ps_ss = k.ps(pfx + "ps_ss", [128, TT])
    ps_g = [k.ps(pfx + "ps_g%d" % i, [128, TT]) for i in range(2)]
    ps_u = [k.ps(pfx + "ps_u%d" % i, [128, TT]) for i in range(2)]
    n_o = 2 if out_tm is not None else 3
    ps_o = [k.ps(pfx + "ps_o%d" % i, [128, TT]) for i in range(n_o)]
    if out_tm is not None:
        NSUB = TT // 128
        otm = [k.sb(pfx + "otm%d" % i, [128, D], F32) for i in range(2)]
        och = [k.s.chan(pfx + "otm%d" % i) for i in range(2)]
        ps_t = k.ps(pfx + "ps_t", [128, 512])

    xin_v = xin.ap.rearrange("(c p) s -> p c s", p=128)
    if out_tm is None:
        xout_v = xout.ap.rearrange("(c p) s -> p c s", p=128)

    def load(t):
        sl = t % 2
        k.dma(xs[sl][:], xin_v[:, :, t * TT:(t + 1) * TT], xch[sl], (xin.b(t),), (xs[sl].b,))

    def front_a(t):
        x = xs[t % 2]
        k.act(sq[:], x[:], AF.Square, (x.b,), (sq.b,))
        for c in range(KC):
            k.mm(ps_ss[:], cst["ones"][:], sq[:, c, :], c == 0, c == KC - 1,
                 (cst["ones"].b, sq.b), (ps_ss.b,))
        k.act(rstd[:], ps_ss[:], AF.Ln, (ps_ss.b,), (rstd.b,), bias=EPS, scale=1.0 / D)
        k.act(rstd[:], rstd[:], AF.Exp, (rstd.b,), (rstd.b,), scale=-0.5)

    def front_h(t, c):
        x, hh = xs[t % 2], h2[t % 2]
        k.stt("dve", hh[:, c, :], x[:, c, :], g[:, c:c + 1], rstd[:], ALU.mult, ALU.mult,
              (x.b, g.b, rstd.b), (hh.b,))

    load(0)
    front_a(0)
    for c in range(KC):
        front_h(0, c)
    ntr = 0
    for t in range(NT):
        x = xs[t % 2]
        h = h2[t % 2]
        if t + 1 < NT:
            load(t + 1)
        for hc in range(HC):
            pg, pu, sgb = ps_g[hc % 2], ps_u[hc % 2], sg[hc % 2]
            for c in range(KC):
                k.mm(pg[:], wgu[:, c, hc * 128:(hc + 1) * 128], h[:, c, :], c == 0, c == KC - 1,
                     (wgu.tok(hc * 128), h.b), (pg.b,))
            for c in range(KC):
                k.mm(pu[:], wgu[:, c, FH + hc * 128:FH + (hc + 1) * 128], h[:, c, :], c == 0, c == KC - 1,
                     (wgu.tok(FH + hc * 128), h.b), (pu.b,))
            k.act(sgb[:], pg[:], AF.Silu, (pg.b,), (sgb.b,))
            k.tt("dve", a[hc][:], sgb[:], pu[:], ALU.mult, (sgb.b, pu.b), (a[hc].b,))
            if t + 1 < NT:
                if hc == 6:
                    front_a(t + 1)
                if 9 <= hc < 9 + KC:
                    front_h(t + 1, hc - 9)
        for oc in range(KC):
            po = ps_o[oc % n_o]
            for hc in range(HC):
                wtok = wd.toks[hc] if isinstance(wd, WStaged) else wd.tok(oc * 128)
                k.mm(po[:], wd[:, hc, oc * 128:(oc + 1) * 128], a[hc][:], hc == 0, hc == HC - 1,
                     (wtok, a[hc].b), (po.b,))
            k.tt("dve", x[:, oc, :], x[:, oc, :], po[:], ALU.add, (x.b, po.b), (x.b,))
        if out_tm is None:
            k.dma(xout_v[:, :, t * TT:(t + 1) * TT], x[:], sch[t % 2], (x.b,), (xout.b(t),))
        else:
            for sub in range(NSUB):
                o = otm[ntr % 2]
                for half in range(2):
                    for j in range(4):
                        oc = half * 4 + j
                        k.tr(ps_t[:, j * 128:(j + 1) * 128], x[:, oc, sub * 128:(sub + 1) * 128],
                             cst["identf"][:], (x.b, cst["identf"].b), (ps_t.b,))
                    k.copy("act", o[:, half * 512:(half + 1) * 512], ps_t[:], (ps_t.b,), (o.b,))
                r0 = t * TT + sub * 128
                k.dma(out_tm.ap[r0:r0 + 128, :], o[:], och[ntr % 2], (o.b,), (out_tm.b(t),))
                ntr += 1


def phase_proj_res(k, cst, S, TT, yin, xin, xout, w_d, kchunks, pfx, prefetch=None):
    NT = S // TT
    KC = D // 128
    w = W(k, pfx + "w", w_d, kchunks, D, [(0, 256), (256, 768)])
    if prefetch is not None:
        prefetch()
    xs = [k.sb(pfx + "x%d" % i, [128, KC, TT], F32) for i in range(2)]
    ys = [k.sb(pfx + "y%d" % i, [128, kchunks, TT], BF16) for i in range(2)]
    xch = [k.s.chan(pfx + "xl%d" % i) for i in range(2)]
    ych = [k.s.chan(pfx + "yl%d" % i) for i in range(2)]
    sch = [k.s.chan(pfx + "xs%d" % i) for i in range(2)]
    ps_o = [k.ps(pfx + "ps_o%d" % i, [128, TT]) for i in range(2)]
    xin_v = xin.ap.rearrange("(c p) s -> p c s", p=128)
    yin_v = yin.ap.rearrange("(c p) s -> p c s", p=128)
    xout_v = xout.ap.rearrange("(c p) s -> p c s", p=128)

    def load(t):
        sl = t % 2
        k.dma(xs[sl][:], xin_v[:, :, t * TT:(t + 1) * TT], xch[sl], (xin.b(t),), (xs[sl].b,))
        k.dma(ys[sl][:], yin_v[:, :, t * TT:(t + 1) * TT], ych[sl], (yin.b(t),), (ys[sl].b,))

    load(0)
    for t in range(NT):
        x, y = xs[t % 2], ys[t % 2]
        if t + 1 < NT:
            load(t + 1)
        for oc in range(KC):
            po = ps_o[oc % 2]
            for c in range(kchunks):
                k.mm(po[:], w[:, c, oc * 128:(oc + 1) * 128], y[:, c, :], c == 0, c == kchunks - 1,
                     (w.tok(oc * 128), y.b), (po.b,))
            k.tt("dve", x[:, oc, :], x[:, oc, :], po[:], ALU.add, (x.b, po.b), (x.b,))
        k.dma(xout_v[:, :, t * TT:(t + 1) * TT], x[:], sch[t % 2], (x.b,), (xout.b(t),))


def ret_tables(S):
    half = RDK // 2
    inv_freq = (np.float32(10000.0) ** (-(np.arange(half, dtype=np.float32)) / np.float32(half))).astype(np.float32)
    ang = (np.arange(S, dtype=np.float32)[None, :] * inv_freq[:, None]).astype(np.float32)
    tabs = {"cos": np.cos(ang).astype(np.float32), "sin": np.sin(ang).astype(np.float32)}
    gam = 1.0 - 2.0 ** (-5.0 - np.arange(RH, dtype=np.float64))
    t = np.arange(128)
    tt, ss = np.meshgrid(t, t, indexing="ij")
    same = (tt // 64) == (ss // 64)
    lower = (tt // 64) > (ss // 64)
    maskT = np.zeros((128, RH, 128), np.float32)
    qdec = np.zeros((128, RH, 256), np.float32)
    kdec = np.zeros((128, RH), np.float32)
    decI = np.zeros((128, RH, 128), np.float32)
    for h in range(RH):
        g = gam[h]
        M = np.where(same, g ** np.abs(tt - ss), np.where(lower, g ** (tt - ss).clip(0), 0.0))
        maskT[:, h, :] = (M.T / 16.0)
        qdec[:, h, :] = (g ** ((np.arange(256) % 128) + 1.0))[None, :]
        kdec[:, h] = g ** (127.0 - t) / 16.0
        decI[:, h, :] = np.eye(128) * g ** 128.0
    tabs["maskT"] = maskT
    tabs["qdec"] = qdec
    tabs["kdec"] = kdec
    tabs["decI"] = decI.astype(ml_dtypes.bfloat16)
    return tabs


def phase_ret(k, cst, S, x_d, xT_out, yT_out, w_in_d, g_d, gn_d, tabs_d, pfx):
    TT = 256
    NSUB = 2
    NT = S // TT
    KC = D // 128
    VOFF = 2048
    groups = [(0, 256), (1024, 256), (256, 768), (1280, 768), (2048, 1024), (4096, 1024), (3072, 1024), (5120, 1024)]
    w = W(k, pfx + "win", w_in_d, KC, 6144, groups)
    g = k.sb(pfx + "g", [128, KC], F32)
    gnl = k.sb(pfx + "gnl", [128, 16], F32)
    maskT = k.sb(pfx + "maskT", [128, RH * 128], F32)
    qdec = k.sb(pfx + "qdec", [128, RH, 256], F32)
    kdec = k.sb(pfx + "kdec", [128, RH], F32)
    decI = k.sb(pfx + "decI", [128, RH, 128], BF16)
    cch = k.s.chan(pfx + "c")
    k.dma(g[:], g_d, cch, (), (g.b,))
    k.dma(gnl[:], gn_d, cch, (), (gnl.b,))
    k.dma(maskT[:], tabs_d["maskT"], cch, (), (maskT.b,))
    k.dma(qdec[:], tabs_d["qdec"], cch, (), (qdec.b,))
    k.dma(kdec[:], tabs_d["kdec"], cch, (), (kdec.b,))
    k.dma(decI[:], tabs_d["decI"], cch, (), (decI.b,))

    xtm = k.sb(pfx + "xtm", [128, NSUB, D], F32)
    xch = k.s.chan(pfx + "xl")
    cs = k.sb(pfx + "cs", [128, 2, TT], F32)
    csch = k.s.chan(pfx + "cs")
    xT = k.sb(pfx + "xT", [128, KC, TT], F32)
    xsch = k.s.chan(pfx + "xs")
    sq = k.sb(pfx + "sq", [128, KC, TT], BF16)
    h = k.sb(pfx + "h", [128, KC, TT], BF16)
    rstd = k.sb(pfx + "rstd", [128, TT], F32)
    qT = [k.sb(pfx + "qT%d" % i, [128, 2, TT], BF16) for i in range(RH)]
    kT = [k.sb(pfx + "kT%d" % i, [128, 2, TT], BF16) for i in range(RH)]
    qdT = [k.sb(pfx + "qdT%d" % i, [128, 2, TT], BF16) for i in range(RH)]
    rt = [[k.sb(pfx + "rt%d_%d" % (i, j), [128, TT], F32) for j in range(4)] for i in range(2)]
    v = k.sb(pfx + "v", [128, 2048], BF16)
    sg = k.sb(pfx + "sg", [128, 2048], F32)
    kd = k.sb(pfx + "kd", [128, RH * 256], BF16)
    ST = [k.sb(pfx + "ST%d" % i, [128, RH * 128], BF16) for i in range(2)]
    st = [[k.sb(pfx + "st%d_%d" % (hd, i), [128, 2, 512], BF16) for i in range(2)] for hd in range(RH)]
    junk = k.sb(pfx + "junk", [128, 512], BF16)
    ssq = [k.sb(pfx + "ssq%d" % i, [128, 1], F32) for i in range(4)]
    y = [k.sb(pfx + "y%d" % i, [128, 2048], BF16) for i in range(2)]
    yT = k.sb(pfx + "yT", [128, 16, TT], BF16)
    ysch = k.s.chan(pfx + "ys")

    pq = [k.ps(pfx + "pq%d" % i, [128, 512]) for i in range(4)]
    pX = k.ps(pfx + "pX", [128, 512])
    pS = k.ps(pfx + "pS", [128, 512])
    pSt = k.ps(pfx + "pSt", [128, 512])
    pB = k.ps(pfx + "pB", [128, 1024], BF16)
    pst2 = [pSt, pX]

    for hd in range(RH):
        k.memset("pool", st[hd][0][:], 0.0, (st[hd][0].b,))

    x_v = x_d.ap.rearrange("(n p) d -> p n d", p=128)
    xTo_v = xT_out.ap.rearrange("(c p) s -> p c s", p=128)
    yTo_v = yT_out.ap.rearrange("(c p) s -> p c s", p=128)
    cnt = {"pair": 0, "vg": 0, "su": 0, "blk": 0}

    def load_x(t):
        k.dma(xtm[:], x_v[:, t * NSUB:(t + 1) * NSUB, :], xch, (x_d.b(t),), (xtm.b,))

    def load_cs(t):
        k.dma(cs[:, 0, :], tabs_d["cos"][:, t * TT:(t + 1) * TT], csch, (), (cs.b,))
        k.dma(cs[:, 1, :], tabs_d["sin"][:, t * TT:(t + 1) * TT], csch, (), (cs.b,))

    def front_a(t):
        for c2 in range(KC // 2):
            pt_ = pst2[c2 % 2]
            for j in range(2):
                c = c2 * 2 + j
                for sub in range(NSUB):
                    k.tr(pt_[:, j * 256 + sub * 128: j * 256 + (sub + 1) * 128],
                         xtm[:, sub, c * 128:(c + 1) * 128], cst["identf"][:],
                         (xtm.b, cst["identf"].b), (pt_.b,))
            k.copy("dve", xT[:, c2 * 2:c2 * 2 + 2, :], pt_[:].rearrange("p (a b) -> p a b", a=2),
                   (pt_.b,), (xT.b,))
        k.dma(xTo_v[:, :, t * TT:(t + 1) * TT], xT[:], xsch, (xT.b,), (xT_out.b(t),))
        k.act(sq[:], xT[:], AF.Square, (xT.b,), (sq.b,))

    def front_b(t):
        for c in range(KC):
            k.mm(pS[:, 0:TT], cst["ones"][:], sq[:, c, :], c == 0, c == KC - 1,
                 (cst["ones"].b, sq.b), (pS.b,))
        k.act(rstd[:], pS[:, 0:TT], AF.Ln, (pS.b,), (rstd.b,), bias=EPS, scale=1.0 / D)
        k.act(rstd[:], rstd[:], AF.Exp, (rstd.b,), (rstd.b,), scale=-0.5)
        for c in range(KC):
            k.stt("dve", h[:, c, :], xT[:, c, :], g[:, c:c + 1], rstd[:], ALU.mult, ALU.mult,
                  (xT.b, g.b, rstd.b), (h.b,))

    def qk(t):
        cos, sin = cs[:, 0, :], cs[:, 1, :]
        for hd in range(RH):
            for which in range(2):
                dst = qT[hd] if which == 0 else kT[hd]
                i = cnt["pair"] % 2
                cnt["pair"] += 1
                pa, pb = pq[2 * i], pq[2 * i + 1]
                r = rt[i]
                for j, p in enumerate((pa, pb)):
                    col = (which * 8 + hd * 2 + j) * 128
                    for c in range(KC):
                        k.mm(p[:, 0:TT], w[:, c, col:col + 128], h[:, c, :], c == 0, c == KC - 1,
                             (w.tok(col), h.b), (p.b,))
                a_, b_ = pa[:, 0:TT], pb[:, 0:TT]
                k.tt("dve", r[0][:], a_, cos, ALU.mult, (pa.b, cs.b), (r[0].b,))
                k.tt("dve", r[1][:], b_, sin, ALU.mult, (pb.b, cs.b), (r[1].b,))
                k.tt("dve", r[2][:], a_, sin, ALU.mult, (pa.b, cs.b), (r[2].b,))
                k.tt("dve", r[3][:], b_, cos, ALU.mult, (pb.b, cs.b), (r[3].b,))
                k.tt("pool", dst[:, 0, :], r[0][:], r[1][:], ALU.subtract, (r[0].b, r[1].b), (dst.b,))
                k.tt("pool", dst[:, 1, :], r[2][:], r[3][:], ALU.add, (r[2].b, r[3].b), (dst.b,))
                if which == 0:
                    for j in range(2):
                        k.tt("pool", qdT[hd][:, j, :], dst[:, j, :], qdec[:, hd, :], ALU.mult,
                             (dst.b, qdec.b), (qdT[hd].b,))

    def vg(t, sub):
        for cb in range(8):
            p = pq[cnt["vg"] % 2]
            cnt["vg"] += 1
            col = VOFF + cb * 512
            for c in range(KC):
                k.mm(p[:], h[:, c, sub * 128:(sub + 1) * 128], w[:, c, col:col + 512], c == 0, c == KC - 1,
                     (w.tok(col), h.b), (p.b,))
            if cb < 4:
                k.copy("act", v[:, cb * 512:(cb + 1) * 512], p[:], (p.b,), (v.b,))
            else:
                cc = cb - 4
                k.act(sg[:, cc * 512:(cc + 1) * 512], p[:], AF.Silu, (p.b,), (sg.b,))

    def kt(t, sub):
        for hd in range(RH):
            for j in range(2):
                k.tr(pB[:, hd * 256 + j * 128: hd * 256 + (j + 1) * 128],
                     kT[hd][:, j, sub * 128:(sub + 1) * 128], cst["identb"][:],
                     (kT[hd].b, cst["identb"].b), (pB.b,))
        for hd in range(RH):
            k.act(kd[:, hd * 256:(hd + 1) * 256], pB[:, hd * 256:(hd + 1) * 256], AF.Copy,
                  (pB.b, kdec.b), (kd.b,), scale=kdec[:, hd:hd + 1])

    def scores(t, sub):
        tok = slice(sub * 128, (sub + 1) * 128)
        STb = ST[cnt["blk"] % 2]
        for hd in range(RH):
            for j in range(2):
                k.mm(pS[:, hd * 128:(hd + 1) * 128], kT[hd][:, j, tok], qT[hd][:, j, tok], j == 0, j == 1,
                     (kT[hd].b, qT[hd].b), (pS.b,))
        k.tt("dve", STb[:], pS[:], maskT[:], ALU.mult, (pS.b, maskT.b), (STb.b,))

    def core(t, sub):
        tok = slice(sub * 128, (sub + 1) * 128)
        n = cnt["blk"]
        cnt["blk"] += 1
        cur, nxt = n % 2, (n + 1) % 2
        STb = ST[n % 2]
        yb = y[sub]
        for hd in range(RH):
            po = pq[2 + hd % 2]
            vh = v[:, hd * 512:(hd + 1) * 512]
            for j in range(2):
                pst = pst2[cnt["su"] % 2]
                cnt["su"] += 1
                k.mm(pst[:], kd[:, hd * 256 + j * 128: hd * 256 + (j + 1) * 128], vh, True, False,
                     (kd.b, v.b), (pst.b,))
                k.mm(pst[:], decI[:, hd, :], st[hd][cur][:, j, :], False, True,
                     (decI.b, st[hd][cur].b), (pst.b,))
                k.copy("act" if j == 0 else "dve", st[hd][nxt][:, j, :], pst[:], (pst.b,), (st[hd][nxt].b,))
            k.mm(po[:], STb[:, hd * 128:(hd + 1) * 128], vh, True, False, (STb.b, v.b), (po.b,))
            for j in range(2):
                k.mm(po[:], qdT[hd][:, j, tok], st[hd][cur][:, j, :], False, j == 1,
                     (qdT[hd].b, st[hd][cur].b), (po.b,))
            sq_ = ssq[hd]
            k.act(junk[:], po[:], AF.Square, (po.b,), (junk.b, sq_.b), accum_out=sq_[:])
            k.act(sq_[:], sq_[:], AF.Ln, (sq_.b,), (sq_.b,), bias=EPS, scale=1.0 / RDV)
            k.act(sq_[:], sq_[:], AF.Exp, (sq_.b,), (sq_.b,), scale=-0.5)
            k.stt("dve", yb[:, hd * 512:(hd + 1) * 512], po[:], sq_[:, 0:1], sg[:, hd * 512:(hd + 1) * 512],
                  ALU.mult, ALU.mult, (po.b, sq_.b, sg.b), (yb.b,))

    def ytrans(t, sub):
        tok = slice(sub * 128, (sub + 1) * 128)
        yb = y[sub]
        for half in range(2):
            pyt = pq[2 + half]
            pyv = pyt[:].bitcast(BF16)
            for c in range(8):
                cc = half * 8 + c
                k.tr(pyv[:, c * 128:(c + 1) * 128], yb[:, cc * 128:(cc + 1) * 128], cst["identb"][:],
                     (yb.b, cst["identb"].b), (pyt.b,))
            k.tt("dve", yT[:, half * 8:(half + 1) * 8, tok], pyv.rearrange("p (a b) -> p a b", a=8),
                 gnl[:, half * 8:(half + 1) * 8].rearrange("p (a b) -> p a b", b=1).broadcast_to([128, 8, 128]),
                 ALU.mult, (pyt.b, gnl.b), (yT.b,))
        if sub == NSUB - 1:
            k.dma(yTo_v[:, :, t * TT:(t + 1) * TT], yT[:], ysch, (yT.b,), (yT_out.b(t),))

    load_x(0)
    load_cs(0)
    front_a(0)
    if NT > 1:
        load_x(1)
    front_b(0)
    for t in range(NT):
        qk(t)
        if t + 1 < NT:
            load_cs(t + 1)
        kt(t, 0)
        scores(t, 0)
        vg(t, 0)
        if t > 0:
            ytrans(t - 1, 1)
        core(t, 0)
        kt(t, 1)
        scores(t, 1)
        vg(t, 1)
        if t + 1 < NT:
            front_a(t + 1)
            if t + 2 < NT:
                load_x(t + 2)
        ytrans(t, 0)
        if t + 1 < NT:
            front_b(t + 1)
        core(t, 1)
    ytrans(NT - 1, 1)


NEG = -30000.0


def att_bias_table(rel_bias):
    rb = np.asarray(rel_bias, np.float32)
    sl = np.arange(128)[:, None, None]
    j = np.arange(5)[None, :, None]
    tl = np.arange(128)[None, None, :]
    dist = (4 - j) * 128 + tl - sl
    idx = np.clip(dist, -256, 256) + 256
    hi = (tl >= 64).astype(np.int64)
    vis = (j * 128 + sl >= 64 * hi) & ((j - 4) * 128 + sl < 64 * (hi + 1))
    tab = rb[:, idx]
    tab = np.where(vis[None], tab, np.float32(NEG))
    return np.ascontiguousarray(tab.transpose(1, 0, 2, 3)).astype(np.float32)


def phase_att(k, cst, S, xin, xout, w_kv_d, w_q_d, w_o_d, gkv_d, gb_d, kng_d, qng_d, bias_d, bd_d, pfx, weights=None):
    TT = 256
    NSUB = 2
    NT = S // TT
    KC = D // 128
    wkv = W(k, pfx + "wkv", w_kv_d, KC, 2 * D, [(0, 256), (256, 768), (D, 1024)])
    if weights is not None:
        wq, wo = weights
    else:
        wq = W(k, pfx + "wq", w_q_d, KC, D, [(0, D)])
        wo = W(k, pfx + "wo", w_o_d, KC, D, [(0, D)])
    gkv = k.sb(pfx + "gkv", [128, KC], F32)
    gb = k.sb(pfx + "gb", [128, KC], F32)
    kng = k.sb(pfx + "kng", [128, 1], F32)
    qng = k.sb(pfx + "qng", [128, 1], F32)
    bias = k.sb(pfx + "bias", [128, AH, 640], F32)
    bd = k.sb(pfx + "bd", [128, 128], BF16)
    cch = k.s.chan(pfx + "c")
    k.dma(gkv[:], gkv_d, cch, (), (gkv.b,))
    k.dma(gb[:], gb_d, cch, (), (gb.b,))
    k.dma(kng[:], kng_d, cch, (), (kng.b,))
    k.dma(qng[:], qng_d, cch, (), (qng.b,))
    k.dma(bd[:], bd_d, cch, (), (bd.b,))
    bch = k.s.chan(pfx + "bias")
    for hh in range(0, AH, 4):
        k.dma(bias[:, hh:hh + 4, :], bias_d[:, hh:hh + 4, :], bch, (), (bias.b,))

    xs = [k.sb(pfx + "x%d" % i, [128, KC, TT], F32) for i in range(2)]
    xch = [k.s.chan(pfx + "xl%d" % i) for i in range(2)]
    sch = [k.s.chan(pfx + "xs%d" % i) for i in range(2)]
    sq = k.sb(pfx + "sq", [128, KC, TT], BF16)
    hkv = k.sb(pfx + "hkv", [128, KC, TT], BF16)
    hq = k.sb(pfx + "hq", [128, KC, TT], BF16)
    rstd = k.sb(pfx + "rstd", [128, TT], F32)
    kTr = [k.sb(pfx + "kTr%d" % i, [128, KC, TT], BF16) for i in range(4)]
    vr = [k.sb(pfx + "vr%d" % i, [128, AH, 65], BF16) for i in range(8)]
    qT = k.sb(pfx + "qT", [128, KC, TT], BF16)
    NB = 2
    NBJ = 5
    sqk = [k.sb(pfx + "sqk%d" % i, [128, TT], BF16) for i in range(NBJ)]
    rk = [k.sb(pfx + "rk%d" % i, [128, TT], F32) for i in range(NBJ)]
    NS = 3
    tmp = [k.sb(pfx + "tmp%d" % i, [128, 640], F32) for i in range(NS)]
    PT = [k.sb(pfx + "PT%d" % i, [128, 640], BF16) for i in range(NS)]
    rec = [k.sb(pfx + "rec%d" % i, [128, 4], F32) for i in range(2)]
    on = [k.sb(pfx + "on%d" % i, [128, D], BF16) for i in range(2)]
    oT = k.sb(pfx + "oT", [128, KC, TT], BF16)

    pScT = [k.ps(pfx + "pSc%d" % i, [128, 1024]) for i in range(NS)]
    pSc = [p_[:] for p_ in pScT]
    pScb = [p_.b for p_ in pScT]
    pp = [k.ps(pfx + "pp%d" % i, [128, 512]) for i in range(NB)]
    pO = pp
    pj = pp + pScT

    for i in range(8):
        k.memset("pool", vr[i][:], 1.0, (vr[i].b,))

    xin_v = xin.ap.rearrange("(c p) s -> p c s", p=128)
    xout_v = xout.ap.rearrange("(c p) s -> p c s", p=128)

    def load(t):
        sl = t % 2
        k.dma(xs[sl][:], xin_v[:, :, t * TT:(t + 1) * TT], xch[sl], (xin.b(t),), (xs[sl].b,))

    def pipeline(stages, skew):
        n = len(stages)
        for i in range(n + skew):
            if i < n:
                stages[i][0]()
            if i - skew >= 0:
                stages[i - skew][1]()

    cnt = {"nb": 0, "nj": 0}

    def front_a(t):
        x = xs[t % 2]
        k.act(sq[:], x[:], AF.Square, (x.b,), (sq.b,))
        p0 = pp[cnt["nb"] % NB]
        cnt["nb"] += 1
        for c in range(KC):
            k.mm(p0[:, 0:TT], cst["ones"][:], sq[:, c, :], c == 0, c == KC - 1,
                 (cst["ones"].b, sq.b), (p0.b,))
        k.act(rstd[:], p0[:, 0:TT], AF.Ln, (p0.b,), (rstd.b,), bias=EPS, scale=1.0 / D)
        k.act(rstd[:], rstd[:], AF.Exp, (rstd.b,), (rstd.b,), scale=-0.5)

    def front_h(t, i):
        x = xs[t % 2]
        c = i % KC
        dst, gsrc = (hkv, gkv) if i < KC else (hq, gb)
        k.stt("dve", dst[:, c, :], x[:, c, :], gsrc[:, c:c + 1], rstd[:], ALU.mult, ALU.mult,
              (x.b, gsrc.b, rstd.b), (dst.b,))

    def kq(t):
        kTt = kTr[t % 4]
        stages = []
        for which in range(2):
            for kc in range(KC):
                def mk(which=which, kc=kc, i=cnt["nj"]):
                    wsrc, hsrc, gsrc, dst = (wkv, hkv, kng, kTt) if which == 0 else (wq, hq, qng, qT)
                    p, sqb, rkb = pj[i % NBJ], sqk[i % NBJ], rk[i % NBJ]

                    def first():
                        for c in range(KC):
                            k.mm(p[:, 0:TT], wsrc[:, c, kc * 128:(kc + 1) * 128], hsrc[:, c, :], c == 0, c == KC - 1,
                                 (wsrc.tok(kc * 128), hsrc.b), (p.b,))
                        k.act(sqb[:], p[:, 0:TT], AF.Square, (p.b,), (sqb.b,))

                    def second():
                        k.mm(p[:, TT:2 * TT], bd[:], sqb[:], True, True, (bd.b, sqb.b), (p.b,))
                        k.act(rkb[:], p[:, TT:2 * TT], AF.Ln, (p.b,), (rkb.b,), bias=EPS, scale=1.0 / ADH)
                        k.act(rkb[:], rkb[:], AF.Exp, (rkb.b,), (rkb.b,), scale=-0.5)
                        k.stt("dve", dst[:, kc, :], p[:, 0:TT], gsrc[:, 0:1], rkb[:], ALU.mult, ALU.mult,
                              (p.b, gsrc.b, rkb.b), (dst.b,))

                    return first, second

                stages.append(mk())
                cnt["nj"] += 1
        pipeline(stages, 3)

    def vproj(t):
        for sub in range(NSUB):
            vb = vr[(t * NSUB + sub) % 8]
            for half in range(2):
                p = pj[cnt["nj"] % NBJ]
                cnt["nj"] += 1
                col = D + half * 512
                for c in range(KC):
                    k.mm(p[:, 0:512], hkv[:, c, sub * 128:(sub + 1) * 128], wkv[:, c, col:col + 512], c == 0, c == KC - 1,
                         (wkv.tok(col), hkv.b), (p.b,))
                k.copy("act", vb[:, half * 8:(half + 1) * 8, 0:64], p[:, 0:512].rearrange("p (a b) -> p a b", a=8),
                       (p.b,), (vb.b,))

    def att(t):
        stages = []
        nb = cnt["nb"]
        for sub in range(NSUB):
            for h in range(AH):
                def mk(sub=sub, h=h, i=len(stages), g=nb + (sub * AH + h) // 4):
                    qb = t * NSUB + sub
                    tok = slice(sub * 128, (sub + 1) * 128)
                    j0 = max(0, 4 - qb)
                    onb = on[qb % 2]
                    kc, e = h // 2, h % 2
                    prt = slice(e * 64, (e + 1) * 64)
                    ii = i - (1 if (sub == 1 and t + 1 < NT) else 0)
                    ps, psb, tb, pt = pSc[ii % NS], pScb[ii % NS], tmp[ii % NS], PT[ii % NS]
                    po = pO[g % NB]
                    cs_ = slice(j0 * 128, 640)

                    def first():
                        for j in range(j0, 5):
                            kb = qb - 4 + j
                            kt = kTr[(kb // 2) % 4]
                            ktok = slice((kb % 2) * 128, (kb % 2) * 128 + 128)
                            k.mm(ps[:, j * 128:(j + 1) * 128], kt[prt, kc, ktok], qT[prt, kc, tok], True, True,
                                 (kt.b, qT.b), (psb,))
                        k.stt("dve", tb[:, cs_], ps[:, cs_], 0.125, bias[:, h, cs_], ALU.mult, ALU.add,
                              (psb, bias.b), (tb.b,))
                        k.act(pt[:, cs_], tb[:, cs_], AF.Exp, (tb.b,), (pt.b,))
                        if sub == 1 and t + 1 < NT:
                            front_h(t + 1, h)

                    def second():
                        hl = h % 4
                        for j in range(j0, 5):
                            kb = qb - 4 + j
                            k.mm(po[:, hl * 65:(hl + 1) * 65], pt[:, j * 128:(j + 1) * 128], vr[kb % 8][:, h, :],
                                 j == j0, j == 4, (pt.b, vr[kb % 8].b), (po.b,))
                        if hl == 3:
                            rc = rec[(h // 4) % 2]
                            pv = po[:, 0:260].rearrange("p (a b) -> p a b", a=4)
                            k.recip(rc[:].rearrange("p (a b) -> p a b", b=1), pv[:, :, 64:65], (po.b,), (rc.b,))
                            h4 = h // 4
                            k.tt("dve", onb[:, h4 * 256:(h4 + 1) * 256].rearrange("p (a b) -> p a b", a=4),
                                 pv[:, :, 0:64],
                                 rc[:].rearrange("p (a b) -> p a b", b=1).broadcast_to([128, 4, 64]), ALU.mult,
                                 (po.b, rc.b), (onb.b,))
                        if h == AH - 1:
                            pbk = pp[(g + 1) % NB]
                            pbv = pbk[:].bitcast(BF16)
                            for c in range(KC):
                                k.tr(pbv[:, c * 128:(c + 1) * 128], onb[:, c * 128:(c + 1) * 128], cst["identb"][:],
                                     (onb.b, cst["identb"].b), (pbk.b,))
                            k.copy("act", oT[:, :, tok], pbv.rearrange("p (a b) -> p a b", a=8), (pbk.b,), (oT.b,))

                    return first, second

                stages.append(mk())
            if sub == 0 and t + 1 < NT:
                stages.append((lambda: front_a(t + 1), lambda: None))
        cnt["nb"] = nb + (NSUB * AH) // 4
        pipeline(stages, 2)

    def wo_res(t):
        x = xs[t % 2]
        for oc in range(KC):
            p = pj[cnt["nj"] % NBJ]
            cnt["nj"] += 1
            for c in range(KC):
                k.mm(p[:, 0:TT], wo[:, c, oc * 128:(oc + 1) * 128], oT[:, c, :], c == 0, c == KC - 1,
                     (wo.tok(oc * 128), oT.b), (p.b,))
            k.tt("dve", x[:, oc, :], x[:, oc, :], p[:, 0:TT], ALU.add, (x.b, p.b), (x.b,))
        k.dma(xout_v[:, :, t * TT:(t + 1) * TT], x[:], sch[t % 2], (x.b,), (xout.b(t),))

    load(0)
    if NT > 1:
        load(1)
    front_a(0)
    for i in range(2 * KC):
        front_h(0, i)
    kq(0)
    for t in range(NT):
        vproj(t)
        att(t)
        if t + 1 < NT:
            kq(t + 1)
        wo_res(t)
        if t + 2 < NT:
            load(t + 2)


def _lay(vec, c):
    return np.ascontiguousarray(np.asarray(vec, np.float32).reshape(c, 128).T)


def build_program(S=SEQ):
    nc = bass.Bass("TRN2", target_bir_lowering=False)

    def ext(name, shape, dt=F32):
        return nc.dram_tensor(name, list(shape), dt, kind="ExternalInput").ap()

    x_d = DT(ext("x", [S, D]), "x")
    a_w_in = ext("a_w_in", [D, 6144])
    a_w_o = ext("a_w_o", [2048, D])
    a_w_gu = ext("a_w_gu", [D, 2 * FH])
    a_w_down = ext("a_w_down", [FH, D])
    w_kv = ext("w_kv", [D, 2 * D])
    b_w_q = ext("b_w_q", [D, D])
    b_w_o = ext("b_w_o", [D, D])
    b_w_gu = ext("b_w_gu", [D, 2 * FH])
    b_w_down = ext("b_w_down", [FH, D])
    a_norm_g = ext("a_norm_g", [128, 8])
    a_gn_g = ext("a_gn_g", [128, 16])
    a_ffn_norm_g = ext("a_ffn_norm_g", [128, 8])
    kv_norm_g = ext("kv_norm_g", [128, 8])
    b_norm_g = ext("b_norm_g", [128, 8])
    b_ffn_norm_g = ext("b_ffn_norm_g", [128, 8])
    kng = ext("kng", [128, 1])
    qng = ext("qng", [128, 1])
    bias_d = ext("att_bias", [128, AH, 640])
    bd_d = ext("bd", [128, 128], BF16)
    ones_d = ext("ones", [128, 128], BF16)
    identf_d = ext("identf", [128, 128])
    identb_d = ext("identb", [128, 128], BF16)
    tabs_d = {
        "cos": ext("t_cos", [128, S]), "sin": ext("t_sin", [128, S]),
        "maskT": ext("t_maskT", [128, RH * 128]), "qdec": ext("t_qdec", [128, RH, 256]),
        "kdec": ext("t_kdec", [128, RH]), "decI": ext("t_decI", [128, RH, 128], BF16),
    }
    out = DT(nc.dram_tensor("out", [S, D], F32, kind="ExternalOutput").ap(), "out")
    xT0 = DT(nc.dram_tensor("xT0", [D, S], F32).ap(), "xT0")
    xT1 = DT(nc.dram_tensor("xT1", [D, S], F32).ap(), "xT1")
    xT2 = DT(nc.dram_tensor("xT2", [D, S], F32).ap(), "xT2")
    xT3 = DT(nc.dram_tensor("xT3", [D, S], F32).ap(), "xT3")
    yT = DT(nc.dram_tensor("yT", [2048, S], BF16).ap(), "yT")

    k = K(nc)
    TT = 256
    k.begin()
    cst = load_consts(k, ones_d, identf_d, identb_d)
    phase_ret(k, cst, S, x_d, xT0, yT, a_w_in, a_norm_g, a_gn_g, tabs_d, "r_")
    k.end()
    k.begin()
    cst = load_consts(k, ones_d, identf_d, identb_d)
    carried = {}

    def prefetch_ffn0():
        carried["w"] = ffn_weights(k, "fa_", a_w_gu, a_w_down, carry=True, only="gu")

    phase_proj_res(k, cst, S, 512, yT, xT0, xT1, a_w_o, 16, "o_", prefetch=prefetch_ffn0)
    k.end()
    k.begin()
    cst = load_consts(k, ones_d, identf_d, identb_d)
    def prefetch_att():
        carried["qo"] = (W(k, "a_wq", b_w_q, 8, D, [(0, D)], carry=True), W(k, "a_wo", b_w_o, 8, D, [(0, D)], carry=True))

    phase_ffn(k, cst, S, TT, xT1, xT2, a_w_gu, a_w_down, a_ffn_norm_g, "fa_", weights=carried["w"],
              prefetch=prefetch_att)
    k.end()
    k.release(carried["w"].t)
    k.begin()
    cst = load_consts(k, ones_d, identf_d, identb_d)
    phase_att(k, cst, S, xT2, xT3, w_kv, b_w_q, b_w_o, kv_norm_g, b_norm_g, kng, qng, bias_d, bd_d, "a_",
              weights=carried["qo"])
    k.end()
    k.release(carried["qo"][0].t)
    k.release(carried["qo"][1].t)
    k.begin()
    cst = load_consts(k, ones_d, identf_d, identb_d)
    phase_ffn(k, cst, S, TT, xT3, None, b_w_gu, b_w_down, b_ffn_norm_g, "fb_", out_tm=out)
    k.end()
    return nc


def kernel(x, a_norm_g, a_w_in, a_gn_g, a_w_o, a_ffn_norm_g, a_w_gu, a_w_down,
           kv_norm_g, w_kv, k_norm_g,
           b_norm_g, b_w_q, b_q_norm_g, b_rel_bias, b_w_o, b_ffn_norm_g, b_w_gu, b_w_down):
    f = lambda a: np.ascontiguousarray(np.asarray(a, dtype=np.float32))
    x = f(x)
    B, S, _ = x.shape
    nc = build_program(S)
    tabs = ret_tables(S)
    shared = {
        "a_w_in": f(a_w_in)[0], "a_w_o": f(a_w_o)[0], "a_w_gu": f(a_w_gu)[0], "a_w_down": f(a_w_down)[0],
        "w_kv": f(w_kv), "b_w_q": f(b_w_q)[0], "b_w_o": f(b_w_o)[0], "b_w_gu": f(b_w_gu)[0],
        "b_w_down": f(b_w_down)[0],
        "a_norm_g": _lay(f(a_norm_g)[0], 8), "a_gn_g": _lay(f(a_gn_g)[0], 16),
        "a_ffn_norm_g": _lay(f(a_ffn_norm_g)[0], 8), "kv_norm_g": _lay(f(kv_norm_g), 8),
        "b_norm_g": _lay(f(b_norm_g)[0], 8), "b_ffn_norm_g": _lay(f(b_ffn_norm_g)[0], 8),
        "kng": np.ascontiguousarray(np.tile(f(k_norm_g), 2)[:, None]),
        "qng": np.ascontiguousarray(np.tile(f(b_q_norm_g)[0], 2)[:, None]),
        "att_bias": att_bias_table(f(b_rel_bias)[0]).reshape(128, AH, 640),
        "bd": np.kron(np.eye(2), np.ones((64, 64))).astype(ml_dtypes.bfloat16),
        "ones": np.ones((128, 128), ml_dtypes.bfloat16),
        "identf": np.eye(128, dtype=np.float32),
        "identb": np.eye(128).astype(ml_dtypes.bfloat16),
        "t_cos": tabs["cos"], "t_sin": tabs["sin"],
        "t_maskT": np.ascontiguousarray(tabs["maskT"].reshape(128, RH * 128)),
        "t_qdec": tabs["qdec"], "t_kdec": tabs["kdec"], "t_decI": tabs["decI"],
    }
    in_maps = []
    for b in range(B):
        m = dict(shared)
        m["x"] = x[b]
        in_maps.append(m)
    res = run_bass_kernel_spmd(nc, in_maps, core_ids=list(range(B)))
    return np.stack([np.asarray(r["out"], dtype=np.float32) for r in res.results], axis=0)
```

```python
import math
from contextlib import ExitStack
import numpy as np
import ml_dtypes
import concourse.bass as bass
import concourse.mybir as mybir
from concourse.bass_utils import run_bass_kernel_spmd

F32 = mybir.dt.float32
BF16 = mybir.dt.bfloat16
ALU = mybir.AluOpType
AF = mybir.ActivationFunctionType
AX = mybir.AxisListType

D = 1024
SEQ = 4096
NCORES = 8
EPS = 1e-6
FH = 2816
RH, RDK, RDV = 4, 256, 512
AH, ADH = 16, 64


class Buf:
    __slots__ = ("name", "writers", "readers")

    def __init__(self, name):
        self.name = name
        self.writers = {}
        self.readers = {}


class Chan:
    def __init__(self, name):
        self.name = name
        self.sem = None
        self.sw = False
        self.slot = None
        self.count = 0
        self.last = None


class Op:
    __slots__ = ("eng", "fn", "deps", "sig", "val", "chan", "idx", "key")

    def __init__(self, eng, fn, chan=None):
        self.eng = eng
        self.fn = fn
        self.deps = []
        self.sig = chan is not None
        self.val = None
        self.chan = chan
        self.key = chan if chan is not None else eng


COMPUTE = ("pe", "act", "dve", "pool")
QUEUES = ("pe", "act", "dve", "pool", "sp")


class Sched:
    def __init__(self, nc):
        self.nc = nc
        self.sems = {q: nc.alloc_semaphore(name="sem_" + q) for q in COMPUTE}
        self.semcount = {q: 0 for q in COMPUTE}
        self.dpool = []
        self.swpool = []
        self.nops = 0
        self.last_sw = None
        self.begin()

    def begin(self):
        self.ops = {q: [] for q in QUEUES}
        self.chans = []

    def chan(self, name, sw=False):
        c = Chan(name)
        pool = self.swpool if sw else self.dpool
        i = sum(1 for o in self.chans if o.sw == sw)
        if i >= len(pool):
            pool.append([self.nc.alloc_semaphore(name=("swsem%d" if sw else "dsem%d") % i), 0])
        c.sw = sw
        c.slot = pool[i]
        c.sem = c.slot[0]
        c.count = c.slot[1]
        self.chans.append(c)
        return c

    def _add(self, op, reads, writes):
        eng = op.eng
        key = op.key
        is_dma = op.chan is not None
        deps = {}

        def dep(o):
            k = o.key
            if k not in deps or deps[k].idx < o.idx:
                deps[k] = o

        for b in reads:
            for k, w in b.writers.items():
                if k == key and eng == "pe" and not is_dma:
                    continue
                dep(w)
        for b in writes:
            for k, w in b.writers.items():
                if k == key and not is_dma:
                    continue
                dep(w)
            for k, r in b.readers.items():
                if k == key and not is_dma:
                    continue
                dep(r)
        if is_dma and op.chan.last is not None:
            dep(op.chan.last)
        if is_dma and op.chan.sw:
            if self.last_sw is not None:
                dep(self.last_sw)
            self.last_sw = op
        op.idx = self.nops
        self.nops += 1
        op.deps = list(deps.values())
        for o in op.deps:
            o.sig = True
        for b in reads:
            b.readers[key] = op
        for b in writes:
            b.writers = {key: op}
            b.readers = {}
        if is_dma:
            op.chan.last = op
        self.ops[eng].append(op)
        return op

    def op(self, eng, fn, reads=(), writes=()):
        return self._add(Op(eng, fn), reads, writes)

    def dma(self, queue, out, in_, chan, reads=(), writes=()):
        def fn(e):
            return e.dma_start(out=out, in_=in_)

        return self._add(Op(queue, fn, chan), reads, writes)

    def emit(self):
        nc = self.nc
        sems = self.sems
        for q in QUEUES:
            for o in self.ops[q]:
                if o.chan is not None:
                    o.chan.count += 16
                    o.val = o.chan.count
                elif o.sig:
                    self.semcount[q] += 1
                    o.val = self.semcount[q]
        for c in self.chans:
            c.slot[1] = c.count
        engines = {"pe": "tensor", "act": "scalar", "dve": "vector", "pool": "gpsimd", "sp": "sync"}
        stats = {}

        def run_queue(q):
            def body(e):
                waited = {}
                nw = 0
                for o in self.ops[q]:
                    for d in o.deps:
                        s = d.chan.sem if d.chan is not None else sems[d.eng]
                        if waited.get(s, 0) >= d.val:
                            continue
                        e.wait_ge(s, d.val)
                        nw += 1
                        waited[s] = d.val
                    ins = o.fn(e)
                    if o.chan is not None:
                        ins.then_inc(o.chan.sem, 16)
                    elif o.sig:
                        ins.then_inc(sems[q], 1)
                if q == "sp":
                    for c in self.chans:
                        if c.last is not None and waited.get(c.sem, 0) < c.count:
                            e.wait_ge(c.sem, c.count)
                    for cq in COMPUTE:
                        if self.semcount[cq] > 0:
                            e.wait_ge(sems[cq], self.semcount[cq])
                stats[q] = (len(self.ops[q]), nw)

            return body

        with nc.Block() as block:
            for q in QUEUES:
                getattr(block, engines[q])(run_queue(q))
        self.stats = stats
        self.begin()


class T:
    def __init__(self, k, name, shape, dtype, psum=False, carry=False):
        nc = k.nc
        self.k = k
        if psum:
            self.h = k.es.enter_context(nc.psum_tensor("p%d_%s" % (k.phase, name), shape, dtype))
        else:
            n = 1
            for d in shape[1:]:
                n *= d
            self.nbytes = n * (2 if dtype == BF16 else 4)
            self.off = k._alloc(self.nbytes, top=False)
            self.h = nc.alloc_sbuf_tensor_at("s%d_%s" % (k.phase, name), shape, dtype, offset=self.off)
            if not carry:
                k.local.append(self)
        self.b = Buf(name)
        self.shape = shape

    def __getitem__(self, idx):
        return self.h[idx]


SB_LO, SB_HI = 16512, 229376


class K:
    def __init__(self, nc):
        self.nc = nc
        self.s = Sched(nc)
        self.es = None
        self.phase = 0
        self.free = [[SB_LO, SB_HI]]
        self.local = []

    def _alloc(self, n, top=False):
        n = (n + 63) // 64 * 64
        blocks = reversed(self.free) if top else self.free
        for blk in blocks:
            if blk[1] - blk[0] >= n:
                if top:
                    blk[1] -= n
                    off = blk[1]
                else:
                    off = blk[0]
                    blk[0] += n
                self.free = [b for b in self.free if b[1] > b[0]]
                return off
        raise MemoryError("SBUF exhausted: need %d, free %s" % (n, self.free))

    def _release(self, off, n):
        n = (n + 63) // 64 * 64
        self.free.append([off, off + n])
        self.free.sort()
        merged = []
        for b in self.free:
            if merged and merged[-1][1] == b[0]:
                merged[-1][1] = b[1]
            else:
                merged.append(list(b))
        self.free = merged

    def release(self, t):
        self._release(t.off, t.nbytes)

    def begin(self):
        self.es = ExitStack()
        self.es.__enter__()
        self.phase += 1

    def end(self):
        for q in ("act", "dve", "pool"):
            if not self.s.ops[q]:
                d = self.sb("dummy_" + q, [128, 16], F32)
                self.memset(q if q != "act" else "dve", d[:], 0.0, (d.b,))
        self.s.emit()
        self.es.__exit__(None, None, None)
        self.es = None
        for t in self.local:
            self._release(t.off, t.nbytes)
        self.local = []

    def sb(self, name, shape, dtype, carry=False):
        return T(self, name, shape, dtype, carry=carry)

    def ps(self, name, shape, dtype=F32):
        return T(self, name, shape, dtype, psum=True)

    def mm(self, out, lhsT, rhs, start, stop, reads, writes):
        self.s.op("pe", lambda e: e.matmul(out, lhsT, rhs, start=start, stop=stop), reads, writes)

    def tr(self, out, in_, ident, reads, writes):
        self.s.op("pe", lambda e: e.transpose(out, in_, ident), reads, writes)

    def act(self, out, in_, func, reads, writes, bias=None, scale=None, accum_out=None):
        kw = {}
        if bias is not None:
            kw["bias"] = bias
        if scale is not None:
            kw["scale"] = scale
        if accum_out is not None:
            kw["accum_out"] = accum_out
        self.s.op("act", lambda e: e.activation(out, in_, func, **kw), reads, writes)

    def tt(self, eng, out, in0, in1, op, reads, writes):
        self.s.op(eng, lambda e: e.tensor_tensor(out, in0, in1, op), reads, writes)

    def ts(self, eng, out, in0, s1, s2, op0, op1, reads, writes):
        if op1 is None:
            self.s.op(eng, lambda e: e.tensor_scalar(out, in0, s1, None, op0), reads, writes)
        else:
            self.s.op(eng, lambda e: e.tensor_scalar(out, in0, s1, s2, op0, op1), reads, writes)

    def stt(self, eng, out, in0, scalar, in1, op0, op1, reads, writes):
        self.s.op(eng, lambda e: e.scalar_tensor_tensor(out, in0, scalar, in1, op0, op1), reads, writes)

    def recip(self, out, in_, reads, writes):
        self.s.op("dve", lambda e: e.reciprocal(out, in_), reads, writes)

    def copy(self, eng, out, in_, reads, writes):
        if eng == "act":
            self.s.op("act", lambda e: e.copy(out, in_), reads, writes)
        elif eng == "dve":
            self.s.op(eng, lambda e: e.tensor_scalar(out, in_, 1.0, None, ALU.mult), reads, writes)
        else:
            self.s.op(eng, lambda e: e.tensor_copy(out, in_), reads, writes)

    def memset(self, eng, ap, val, writes):
        self.s.op(eng, lambda e: e.memset(ap, val), (), writes)

    def dma(self, out, in_, chan, reads, writes, queue="sp"):
        self.s.dma(queue, out, in_, chan, reads, writes)


class DT:
    def __init__(self, ap, name):
        self.ap = ap
        self.name = name
        self.bufs = {}

    def b(self, i):
        if i not in self.bufs:
            self.bufs[i] = Buf("%s_%d" % (self.name, i))
        return self.bufs[i]


def load_consts(k, ones_d, identf_d, identb_d):
    c = {}
    c["ones"] = k.sb("ones", [128, 128], BF16)
    c["identf"] = k.sb("identf", [128, 128], F32)
    c["identb"] = k.sb("identb", [128, 128], BF16)
    ch = k.s.chan("const")
    k.dma(c["ones"][:], ones_d, ch, (), (c["ones"].b,))
    k.dma(c["identf"][:], identf_d, ch, (), (c["identf"].b,))
    k.dma(c["identb"][:], identb_d, ch, (), (c["identb"].b,))
    return c


class W:
    def __init__(self, k, name, w_d, kchunks, ncols, groups, carry=False, nch=1):
        self.t = k.sb(name, [128, kchunks, ncols], BF16, carry=carry)
        self.groups = []
        chans = [k.s.chan("%s_%d" % (name, i), sw=True) for i in range(nch)]
        wv = w_d.rearrange("(c p) n -> p c n", p=128)
        for i, (c0, cn) in enumerate(groups):
            b = Buf("%s_g%d" % (name, i))
            self.groups.append((c0, c0 + cn, b))
            k.dma(self.t[:, :, c0:c0 + cn], wv[:, :, c0:c0 + cn], chans[i % nch], (), (b,), queue="pool")

    def __getitem__(self, idx):
        return self.t[idx]

    def tok(self, col):
        for c0, c1, b in self.groups:
            if c0 <= col < c1:
                return b
        raise KeyError(col)


class WStaged:
    def __init__(self, k, name, w_d, kchunks, ncols):
        self.t = k.sb(name, [128, kchunks, ncols], BF16)
        stg = [k.sb("%s_stg%d" % (name, i), [128, ncols], F32) for i in range(2)]
        chs = [k.s.chan("%s_stg%d" % (name, i)) for i in range(2)]
        self.b = Buf(name + "_all")
        self.toks = []
        for c in range(kchunks):
            s_ = stg[c % 2]
            b = Buf("%s_c%d" % (name, c))
            self.toks.append(b)
            k.dma(s_[:], w_d[c * 128:(c + 1) * 128, :], chs[c % 2], (), (s_.b,))
            k.s.op("pool", lambda e, o=self.t[:, c, :], i=s_[:]: e.tensor_copy(o, i), (s_.b,), (b,))

    def __getitem__(self, idx):
        return self.t[idx]

    def tok(self, col):
        return self.toks[-1]


def ffn_weights(k, pfx, w_gu_d, w_down_d, carry=False, hw_down=False, only=None):
    KC, HC = D // 128, FH // 128
    gu_groups = []
    c0 = 0
    for n in (256, 512, 1024, 1024):
        gu_groups.append((c0, n))
        gu_groups.append((FH + c0, n))
        c0 += n
    if only == "gu":
        return W(k, pfx + "wgu", w_gu_d, KC, 2 * FH, gu_groups, carry=carry)
    if only == "down":
        return W(k, pfx + "wd", w_down_d, HC, D, [(0, 512), (512, 512)], carry=carry)
    wgu = W(k, pfx + "wgu", w_gu_d, KC, 2 * FH, gu_groups, carry=carry)
    if hw_down:
        wd = WStaged(k, pfx + "wd", w_down_d, HC, D)
    else:
        wd = W(k, pfx + "wd", w_down_d, HC, D, [(0, 512), (512, 512)], carry=carry)
    return wgu, wd


def phase_ffn(k, cst, S, TT, xin, xout, w_gu_d, w_down_d, g_d, pfx, out_tm=None, weights=None, prefetch=None,
              hw_down=False):
    nc = k.nc
    NT = S // TT
    KC = D // 128
    HC = FH // 128
    if weights is None:
        wgu, wd = ffn_weights(k, pfx, w_gu_d, w_down_d, hw_down=hw_down)
    elif isinstance(weights, tuple):
        wgu, wd = weights
    else:
        wgu = weights
        wd = ffn_weights(k, pfx, w_gu_d, w_down_d, only="down")
    if prefetch is not None:
        prefetch()
    g = k.sb(pfx + "g", [128, KC], F32)
    chg = k.s.chan(pfx + "g")
    k.dma(g[:], g_d, chg, (), (g.b,))

    xs = [k.sb(pfx + "x%d" % i, [128, KC, TT], F32) for i in range(2)]
    xch = [k.s.chan(pfx + "xl%d" % i) for i in range(2)]
    sch = [k.s.chan(pfx + "xs%d" % i) for i in range(2)]
    sq = k.sb(pfx + "sq", [128, KC, TT], BF16)
    h2 = [k.sb(pfx + "h%d" % i, [128, KC, TT], BF16) for i in range(2)]
    a = [k.sb(pfx + "a%d" % i, [128, TT], BF16) for i in range(HC)]
    sg = [k.sb(pfx + "sg%d" % i, [128, TT], F32) for i in range(2)]
    rstd = k.sb(pfx + "rstd", [128, TT], F32)
    ps_ss = k.ps(pfx + "ps_ss", [128, TT])
    ps_g = [k.ps(pfx + "ps_g%d" % i, [128, TT]) for i in range(2)]
    ps_u = [k.ps(pfx + "ps_u%d" % i, [128, TT]) for i in range(2)]
    ps_o = [k.ps(pfx + "ps_o%d" % i, [128, TT]) for i in range(2)]
    if out_tm is not None:
        NSUB = TT // 128
        otm = [k.sb(pfx + "otm%d" % i, [128, D], F32) for i in range(2)]
        och = [k.s.chan(pfx + "otm%d" % i) for i in range(2)]
        ps_t = k.ps(pfx + "ps_t", [128, 512])

    xin_v = xin.ap.rearrange("(c p) s -> p c s", p=128)
    if out_tm is None:
        xout_v = xout.ap.rearrange("(c p) s -> p c s", p=128)

    def load(t):
        sl = t % 2
        k.dma(xs[sl][:], xin_v[:, :, t * TT:(t + 1) * TT], xch[sl], (xin.b(t),), (xs[sl].b,))

    def front_a(t):
        x = xs[t % 2]
        k.act(sq[:], x[:], AF.Square, (x.b,), (sq.b,))
        for c in range(KC):
            k.mm(ps_ss[:], cst["ones"][:], sq[:, c, :], c == 0, c == KC - 1,
                 (cst["ones"].b, sq.b), (ps_ss.b,))
        k.act(rstd[:], ps_ss[:], AF.Ln, (ps_ss.b,), (rstd.b,), bias=EPS, scale=1.0 / D)
        k.act(rstd[:], rstd[:], AF.Exp, (rstd.b,), (rstd.b,), scale=-0.5)

    def front_h(t, c):
        x, hh = xs[t % 2], h2[t % 2]
        k.stt("dve", hh[:, c, :], x[:, c, :], g[:, c:c + 1], rstd[:], ALU.mult, ALU.mult,
              (x.b, g.b, rstd.b), (hh.b,))

    load(0)
    front_a(0)
    for c in range(KC):
        front_h(0, c)
    ntr = 0
    for t in range(NT):
        x = xs[t % 2]
        h = h2[t % 2]
        if t + 1 < NT:
            load(t + 1)
        for hc in range(HC):
            pg, pu, sgb = ps_g[hc % 2], ps_u[hc % 2], sg[hc % 2]
            for c in range(KC):
                k.mm(pg[:], wgu[:, c, hc * 128:(hc + 1) * 128], h[:, c, :], c == 0, c == KC - 1,
                     (wgu.tok(hc * 128), h.b), (pg.b,))
            for c in range(KC):
                k.mm(pu[:], wgu[:, c, FH + hc * 128:FH + (hc + 1) * 128], h[:, c, :], c == 0, c == KC - 1,
                     (wgu.tok(FH + hc * 128), h.b), (pu.b,))
            k.act(sgb[:], pg[:], AF.Silu, (pg.b,), (sgb.b,))
            k.tt("dve", a[hc][:], sgb[:], pu[:], ALU.mult, (sgb.b, pu.b), (a[hc].b,))
            if t + 1 < NT:
                if hc == 6:
                    front_a(t + 1)
                if 9 <= hc < 9 + KC:
                    front_h(t + 1, hc - 9)
        for oc in range(KC):
            po = ps_o[oc % 2]
            for hc in range(HC):
                wtok = wd.toks[hc] if isinstance(wd, WStaged) else wd.tok(oc * 128)
                k.mm(po[:], wd[:, hc, oc * 128:(oc + 1) * 128], a[hc][:], hc == 0, hc == HC - 1,
                     (wtok, a[hc].b), (po.b,))
            k.tt("dve", x[:, oc, :], x[:, oc, :], po[:], ALU.add, (x.b, po.b), (x.b,))
        if out_tm is None:
            k.dma(xout_v[:, :, t * TT:(t + 1) * TT], x[:], sch[t % 2], (x.b,), (xout.b(t),))
        else:
            for sub in range(NSUB):
                o = otm[ntr % 2]
                for half in range(2):
                    for j in range(4):
                        oc = half * 4 + j
                        k.tr(ps_t[:, j * 128:(j + 1) * 128], x[:, oc, sub * 128:(sub + 1) * 128],
                             cst["identf"][:], (x.b, cst["identf"].b), (ps_t.b,))
                    k.copy("act", o[:, half * 512:(half + 1) * 512], ps_t[:], (ps_t.b,), (o.b,))
                r0 = t * TT + sub * 128
                k.dma(out_tm.ap[r0:r0 + 128, :], o[:], och[ntr % 2], (o.b,), (out_tm.b(t),))
                ntr += 1


def phase_proj_res(k, cst, S, TT, yin, xin, xout, w_d, kchunks, pfx, prefetch=None):
    NT = S // TT
    KC = D // 128
    w = W(k, pfx + "w", w_d, kchunks, D, [(0, 256), (256, 768)])
    if prefetch is not None:
        prefetch()
    xs = [k.sb(pfx + "x%d" % i, [128, KC, TT], F32) for i in range(2)]
    ys = [k.sb(pfx + "y%d" % i, [128, kchunks, TT], BF16) for i in range(2)]
    xch = [k.s.chan(pfx + "xl%d" % i) for i in range(2)]
    ych = [k.s.chan(pfx + "yl%d" % i) for i in range(2)]
    sch = [k.s.chan(pfx + "xs%d" % i) for i in range(2)]
    ps_o = [k.ps(pfx + "ps_o%d" % i, [128, TT]) for i in range(2)]
    xin_v = xin.ap.rearrange("(c p) s -> p c s", p=128)
    yin_v = yin.ap.rearrange("(c p) s -> p c s", p=128)
    xout_v = xout.ap.rearrange("(c p) s -> p c s", p=128)

    def load(t):
        sl = t % 2
        k.dma(xs[sl][:], xin_v[:, :, t * TT:(t + 1) * TT], xch[sl], (xin.b(t),), (xs[sl].b,))
        k.dma(ys[sl][:], yin_v[:, :, t * TT:(t + 1) * TT], ych[sl], (yin.b(t),), (ys[sl].b,))

    load(0)
    for t in range(NT):
        x, y = xs[t % 2], ys[t % 2]
        if t + 1 < NT:
            load(t + 1)
        for oc in range(KC):
            po = ps_o[oc % 2]
            for c in range(kchunks):
                k.mm(po[:], w[:, c, oc * 128:(oc + 1) * 128], y[:, c, :], c == 0, c == kchunks - 1,
                     (w.tok(oc * 128), y.b), (po.b,))
            k.tt("dve", x[:, oc, :], x[:, oc, :], po[:], ALU.add, (x.b, po.b), (x.b,))
        k.dma(xout_v[:, :, t * TT:(t + 1) * TT], x[:], sch[t % 2], (x.b,), (xout.b(t),))


def ret_tables(S):
    half = RDK // 2
    inv_freq = (np.float32(10000.0) ** (-(np.arange(half, dtype=np.float32)) / np.float32(half))).astype(np.float32)
    ang = (np.arange(S, dtype=np.float32)[None, :] * inv_freq[:, None]).astype(np.float32)
    tabs = {"cos": np.cos(ang).astype(np.float32), "sin": np.sin(ang).astype(np.float32)}
    gam = 1.0 - 2.0 ** (-5.0 - np.arange(RH, dtype=np.float64))
    t = np.arange(128)
    tt, ss = np.meshgrid(t, t, indexing="ij")
    same = (tt // 64) == (ss // 64)
    lower = (tt // 64) > (ss // 64)
    maskT = np.zeros((128, RH, 128), np.float32)
    qdec = np.zeros((128, RH, 256), np.float32)
    kdec = np.zeros((128, RH), np.float32)
    decI = np.zeros((128, RH, 128), np.float32)
    for h in range(RH):
        g = gam[h]
        M = np.where(same, g ** np.abs(tt - ss), np.where(lower, g ** (tt - ss).clip(0), 0.0))
        maskT[:, h, :] = (M.T / 16.0)
        qdec[:, h, :] = (g ** ((np.arange(256) % 128) + 1.0))[None, :]
        kdec[:, h] = g ** (127.0 - t) / 16.0
        decI[:, h, :] = np.eye(128) * g ** 128.0
    tabs["maskT"] = maskT
    tabs["qdec"] = qdec
    tabs["kdec"] = kdec
    tabs["decI"] = decI.astype(ml_dtypes.bfloat16)
    return tabs


def phase_ret(k, cst, S, x_d, xT_out, yT_out, w_in_d, g_d, gn_d, tabs_d, pfx):
    TT = 256
    NSUB = 2
    NT = S // TT
    KC = D // 128
    VOFF = 2048
    groups = [(0, 256), (1024, 256), (256, 768), (1280, 768), (2048, 1024), (4096, 1024), (3072, 1024), (5120, 1024)]
    w = W(k, pfx + "win", w_in_d, KC, 6144, groups)
    g = k.sb(pfx + "g", [128, KC], F32)
    gnl = k.sb(pfx + "gnl", [128, 16], F32)
    maskT = k.sb(pfx + "maskT", [128, RH * 128], F32)
    qdec = k.sb(pfx + "qdec", [128, RH, 256], F32)
    kdec = k.sb(pfx + "kdec", [128, RH], F32)
    decI = k.sb(pfx + "decI", [128, RH, 128], BF16)
    cch = k.s.chan(pfx + "c")
    k.dma(g[:], g_d, cch, (), (g.b,))
    k.dma(gnl[:], gn_d, cch, (), (gnl.b,))
    k.dma(maskT[:], tabs_d["maskT"], cch, (), (maskT.b,))
    k.dma(qdec[:], tabs_d["qdec"], cch, (), (qdec.b,))
    k.dma(kdec[:], tabs_d["kdec"], cch, (), (kdec.b,))
    k.dma(decI[:], tabs_d["decI"], cch, (), (decI.b,))

    xtm = k.sb(pfx + "xtm", [128, NSUB, D], F32)
    xch = k.s.chan(pfx + "xl")
    cs = k.sb(pfx + "cs", [128, 2, TT], F32)
    csch = k.s.chan(pfx + "cs")
    xT = k.sb(pfx + "xT", [128, KC, TT], F32)
    xsch = k.s.chan(pfx + "xs")
    sq = k.sb(pfx + "sq", [128, KC, TT], BF16)
    h = k.sb(pfx + "h", [128, KC, TT], BF16)
    rstd = k.sb(pfx + "rstd", [128, TT], F32)
    qT = [k.sb(pfx + "qT%d" % i, [128, 2, TT], BF16) for i in range(RH)]
    kT = [k.sb(pfx + "kT%d" % i, [128, 2, TT], BF16) for i in range(RH)]
    qdT = [k.sb(pfx + "qdT%d" % i, [128, 2, TT], BF16) for i in range(RH)]
    rt = [[k.sb(pfx + "rt%d_%d" % (i, j), [128, TT], F32) for j in range(4)] for i in range(2)]
    v = k.sb(pfx + "v", [128, 2048], BF16)
    sg = k.sb(pfx + "sg", [128, 2048], F32)
    kd = k.sb(pfx + "kd", [128, RH * 256], BF16)
    ST = [k.sb(pfx + "ST%d" % i, [128, RH * 128], BF16) for i in range(2)]
    st = [[k.sb(pfx + "st%d_%d" % (hd, i), [128, 2, 512], BF16) for i in range(2)] for hd in range(RH)]
    junk = k.sb(pfx + "junk", [128, 512], BF16)
    ssq = [k.sb(pfx + "ssq%d" % i, [128, 1], F32) for i in range(4)]
    y = [k.sb(pfx + "y%d" % i, [128, 2048], BF16) for i in range(2)]
    yT = k.sb(pfx + "yT", [128, 16, TT], BF16)
    ysch = k.s.chan(pfx + "ys")

    pq = [k.ps(pfx + "pq%d" % i, [128, 512]) for i in range(4)]
    pX = k.ps(pfx + "pX", [128, 512])
    pS = k.ps(pfx + "pS", [128, 512])
    pSt = k.ps(pfx + "pSt", [128, 512])
    pB = k.ps(pfx + "pB", [128, 1024], BF16)
    pst2 = [pSt, pX]

    for hd in range(RH):
        k.memset("pool", st[hd][0][:], 0.0, (st[hd][0].b,))

    x_v = x_d.ap.rearrange("(n p) d -> p n d", p=128)
    xTo_v = xT_out.ap.rearrange("(c p) s -> p c s", p=128)
    yTo_v = yT_out.ap.rearrange("(c p) s -> p c s", p=128)
    cnt = {"pair": 0, "vg": 0, "su": 0, "blk": 0}

    def load_x(t):
        k.dma(xtm[:], x_v[:, t * NSUB:(t + 1) * NSUB, :], xch, (x_d.b(t),), (xtm.b,))

    def load_cs(t):
        k.dma(cs[:].rearrange("p a b -> p (a b)"), tabs_d["cs"][:, t * 2 * TT:(t + 1) * 2 * TT], csch, (), (cs.b,))

    def front_a(t):
        for c2 in range(KC // 2):
            pt_ = pst2[c2 % 2]
            for j in range(2):
                c = c2 * 2 + j
                for sub in range(NSUB):
                    k.tr(pt_[:, j * 256 + sub * 128: j * 256 + (sub + 1) * 128],
                         xtm[:, sub, c * 128:(c + 1) * 128], cst["identf"][:],
                         (xtm.b, cst["identf"].b), (pt_.b,))
            k.copy("dve", xT[:, c2 * 2:c2 * 2 + 2, :], pt_[:].rearrange("p (a b) -> p a b", a=2),
                   (pt_.b,), (xT.b,))
        k.dma(xTo_v[:, :, t * TT:(t + 1) * TT], xT[:], xsch, (xT.b,), (xT_out.b(t),))
        k.act(sq[:], xT[:], AF.Square, (xT.b,), (sq.b,))

    def front_b(t):
        for c in range(KC):
            k.mm(pS[:, 0:TT], cst["ones"][:], sq[:, c, :], c == 0, c == KC - 1,
                 (cst["ones"].b, sq.b), (pS.b,))
        k.act(rstd[:], pS[:, 0:TT], AF.Ln, (pS.b,), (rstd.b,), bias=EPS, scale=1.0 / D)
        k.act(rstd[:], rstd[:], AF.Exp, (rstd.b,), (rstd.b,), scale=-0.5)
        for c in range(KC):
            k.stt("dve", h[:, c, :], xT[:, c, :], g[:, c:c + 1], rstd[:], ALU.mult, ALU.mult,
                  (xT.b, g.b, rstd.b), (h.b,))

    def qk(t):
        cos, sin = cs[:, 0, :], cs[:, 1, :]
        for hd in range(RH):
            for which in range(2):
                dst = qT[hd] if which == 0 else kT[hd]
                i = cnt["pair"] % 2
                cnt["pair"] += 1
                pa, pb = pq[2 * i], pq[2 * i + 1]
                r = rt[i]
                for j, p in enumerate((pa, pb)):
                    col = (which * 8 + hd * 2 + j) * 128
                    for c in range(KC):
                        k.mm(p[:, 0:TT], w[:, c, col:col + 128], h[:, c, :], c == 0, c == KC - 1,
                             (w.tok(col), h.b), (p.b,))
                a_, b_ = pa[:, 0:TT], pb[:, 0:TT]
                k.tt("dve", r[0][:], a_, cos, ALU.mult, (pa.b, cs.b), (r[0].b,))
                k.tt("dve", r[1][:], b_, sin, ALU.mult, (pb.b, cs.b), (r[1].b,))
                k.tt("dve", r[2][:], a_, sin, ALU.mult, (pa.b, cs.b), (r[2].b,))
                k.tt("dve", r[3][:], b_, cos, ALU.mult, (pb.b, cs.b), (r[3].b,))
                k.tt("pool", dst[:, 0, :], r[0][:], r[1][:], ALU.subtract, (r[0].b, r[1].b), (dst.b,))
                k.tt("pool", dst[:, 1, :], r[2][:], r[3][:], ALU.add, (r[2].b, r[3].b), (dst.b,))
                if which == 0:
                    for j in range(2):
                        k.tt("pool", qdT[hd][:, j, :], dst[:, j, :], qdec[:, hd, :], ALU.mult,
                             (dst.b, qdec.b), (qdT[hd].b,))

    def vg(t, sub):
        for cb in range(8):
            p = pq[cnt["vg"] % 2]
            cnt["vg"] += 1
            col = VOFF + cb * 512
            for c in range(KC):
                k.mm(p[:], h[:, c, sub * 128:(sub + 1) * 128], w[:, c, col:col + 512], c == 0, c == KC - 1,
                     (w.tok(col), h.b), (p.b,))
            if cb < 4:
                k.copy("act", v[:, cb * 512:(cb + 1) * 512], p[:], (p.b,), (v.b,))
            else:
                cc = cb - 4
                k.act(sg[:, cc * 512:(cc + 1) * 512], p[:], AF.Silu, (p.b,), (sg.b,))

    def kt(t, sub):
        for hd in range(RH):
            for j in range(2):
                k.tr(pB[:, hd * 256 + j * 128: hd * 256 + (j + 1) * 128],
                     kT[hd][:, j, sub * 128:(sub + 1) * 128], cst["identb"][:],
                     (kT[hd].b, cst["identb"].b), (pB.b,))
        for hd in range(RH):
            k.act(kd[:, hd * 256:(hd + 1) * 256], pB[:, hd * 256:(hd + 1) * 256], AF.Copy,
                  (pB.b, kdec.b), (kd.b,), scale=kdec[:, hd:hd + 1])

    def scores(t, sub):
        tok = slice(sub * 128, (sub + 1) * 128)
        STb = ST[cnt["blk"] % 2]
        for hd in range(RH):
            for j in range(2):
                k.mm(pS[:, hd * 128:(hd + 1) * 128], kT[hd][:, j, tok], qT[hd][:, j, tok], j == 0, j == 1,
                     (kT[hd].b, qT[hd].b), (pS.b,))
        k.tt("dve", STb[:], pS[:], maskT[:], ALU.mult, (pS.b, maskT.b), (STb.b,))

    def core(t, sub):
        tok = slice(sub * 128, (sub + 1) * 128)
        n = cnt["blk"]
        cnt["blk"] += 1
        cur, nxt = n % 2, (n + 1) % 2
        STb = ST[n % 2]
        yb = y[sub]
        for hd in range(RH):
            po = pq[2 + hd % 2]
            vh = v[:, hd * 512:(hd + 1) * 512]
            for j in range(2):
                pst = pst2[cnt["su"] % 2]
                cnt["su"] += 1
                k.mm(pst[:], kd[:, hd * 256 + j * 128: hd * 256 + (j + 1) * 128], vh, True, False,
                     (kd.b, v.b), (pst.b,))
                k.mm(pst[:], decI[:, hd, :], st[hd][cur][:, j, :], False, True,
                     (decI.b, st[hd][cur].b), (pst.b,))
                k.copy("act" if j == 0 else "dve", st[hd][nxt][:, j, :], pst[:], (pst.b,), (st[hd][nxt].b,))
            k.mm(po[:], STb[:, hd * 128:(hd + 1) * 128], vh, True, False, (STb.b, v.b), (po.b,))
            for j in range(2):
                k.mm(po[:], qdT[hd][:, j, tok], st[hd][cur][:, j, :], False, j == 1,
                     (qdT[hd].b, st[hd][cur].b), (po.b,))
            sq_ = ssq[hd]
            k.act(junk[:], po[:], AF.Square, (po.b,), (junk.b, sq_.b), accum_out=sq_[:])
            k.act(sq_[:], sq_[:], AF.Ln, (sq_.b,), (sq_.b,), bias=EPS, scale=1.0 / RDV)
            k.act(sq_[:], sq_[:], AF.Exp, (sq_.b,), (sq_.b,), scale=-0.5)
            k.stt("dve", yb[:, hd * 512:(hd + 1) * 512], po[:], sq_[:, 0:1], sg[:, hd * 512:(hd + 1) * 512],
                  ALU.mult, ALU.mult, (po.b, sq_.b, sg.b), (yb.b,))

    def ytrans(t, sub):
        tok = slice(sub * 128, (sub + 1) * 128)
        yb = y[sub]
        for half in range(2):
            pyt = pq[2 + half]
            pyv = pyt[:].bitcast(BF16)
            for c in range(8):
                cc = half * 8 + c
                k.tr(pyv[:, c * 128:(c + 1) * 128], yb[:, cc * 128:(cc + 1) * 128], cst["identb"][:],
                     (yb.b, cst["identb"].b), (pyt.b,))
            k.tt("dve", yT[:, half * 8:(half + 1) * 8, tok], pyv.rearrange("p (a b) -> p a b", a=8),
                 gnl[:, half * 8:(half + 1) * 8].rearrange("p (a b) -> p a b", b=1).broadcast_to([128, 8, 128]),
                 ALU.mult, (pyt.b, gnl.b), (yT.b,))
        if sub == NSUB - 1:
            k.dma(yTo_v[:, :, t * TT:(t + 1) * TT], yT[:], ysch, (yT.b,), (yT_out.b(t),))

    load_x(0)
    load_cs(0)
    front_a(0)
    if NT > 1:
        load_x(1)
    front_b(0)
    for t in range(NT):
        qk(t)
        if t + 1 < NT:
            load_cs(t + 1)
        kt(t, 0)
        scores(t, 0)
        vg(t, 0)
        if t > 0:
            ytrans(t - 1, 1)
        core(t, 0)
        kt(t, 1)
        scores(t, 1)
        vg(t, 1)
        if t + 1 < NT:
            front_a(t + 1)
            if t + 2 < NT:
                load_x(t + 2)
        ytrans(t, 0)
        if t + 1 < NT:
            front_b(t + 1)
        core(t, 1)
    ytrans(NT - 1, 1)


NEG = -30000.0


def att_bias_table(rel_bias):
    rb = np.asarray(rel_bias, np.float32)
    sl = np.arange(128)[:, None, None]
    j = np.arange(5)[None, :, None]
    tl = np.arange(128)[None, None, :]
    dist = (4 - j) * 128 + tl - sl
    idx = np.clip(dist, -256, 256) + 256
    hi = (tl >= 64).astype(np.int64)
    vis = (j * 128 + sl >= 64 * hi) & ((j - 4) * 128 + sl < 64 * (hi + 1))
    tab = rb[:, idx]
    tab = np.where(vis[None], tab, np.float32(NEG))
    return np.ascontiguousarray(tab.transpose(1, 0, 2, 3)).astype(np.float32)


def phase_att(k, cst, S, xin, xout, w_kv_d, w_q_d, w_o_d, gkv_d, gb_d, kng_d, qng_d, bias_d, bd_d, pfx, weights=None):
    TT = 256
    NSUB = 2
    NT = S // TT
    KC = D // 128
    wkv = W(k, pfx + "wkv", w_kv_d, KC, 2 * D, [(0, 256), (256, 768), (D, 1024)])
    if weights is not None:
        wq, wo = weights
    else:
        wq = W(k, pfx + "wq", w_q_d, KC, D, [(0, D)])
        wo = W(k, pfx + "wo", w_o_d, KC, D, [(0, D)])
    gkv = k.sb(pfx + "gkv", [128, KC], F32)
    gb = k.sb(pfx + "gb", [128, KC], F32)
    kng = k.sb(pfx + "kng", [128, 1], F32)
    qng = k.sb(pfx + "qng", [128, 1], F32)
    bias = k.sb(pfx + "bias", [128, AH, 640], F32)
    bd = k.sb(pfx + "bd", [128, 128], BF16)
    cch = k.s.chan(pfx + "c")
    k.dma(gkv[:], gkv_d, cch, (), (gkv.b,))
    k.dma(gb[:], gb_d, cch, (), (gb.b,))
    k.dma(kng[:], kng_d, cch, (), (kng.b,))
    k.dma(qng[:], qng_d, cch, (), (qng.b,))
    k.dma(bd[:], bd_d, cch, (), (bd.b,))
    bch = k.s.chan(pfx + "bias")
    for hh in range(0, AH, 4):
        k.dma(bias[:, hh:hh + 4, :], bias_d[:, hh:hh + 4, :], bch, (), (bias.b,))

    xs = [k.sb(pfx + "x%d" % i, [128, KC, TT], F32) for i in range(2)]
    xch = [k.s.chan(pfx + "xl%d" % i) for i in range(2)]
    sch = [k.s.chan(pfx + "xs%d" % i) for i in range(2)]
    sq = k.sb(pfx + "sq", [128, KC, TT], BF16)
    hkv = k.sb(pfx + "hkv", [128, KC, TT], BF16)
    hq = k.sb(pfx + "hq", [128, KC, TT], BF16)
    rstd = k.sb(pfx + "rstd", [128, TT], F32)
    kTr = [k.sb(pfx + "kTr%d" % i, [128, KC, TT], BF16) for i in range(4)]
    vr = [k.sb(pfx + "vr%d" % i, [128, AH, 65], BF16) for i in range(8)]
    qT = k.sb(pfx + "qT", [128, KC, TT], BF16)
    NB = 2
    NBJ = 5
    sqk = [k.sb(pfx + "sqk%d" % i, [128, TT], BF16) for i in range(NBJ)]
    rk = [k.sb(pfx + "rk%d" % i, [128, TT], F32) for i in range(NBJ)]
    NS = 3
    tmp = [k.sb(pfx + "tmp%d" % i, [128, 640], F32) for i in range(NS)]
    PT = [k.sb(pfx + "PT%d" % i, [128, 640], BF16) for i in range(NS)]
    rec = [k.sb(pfx + "rec%d" % i, [128, 4], F32) for i in range(2)]
    on = [k.sb(pfx + "on%d" % i, [128, D], BF16) for i in range(2)]
    oT = k.sb(pfx + "oT", [128, KC, TT], BF16)

    pScT = [k.ps(pfx + "pSc%d" % i, [128, 1024]) for i in range(NS)]
    pSc = [p_[:] for p_ in pScT]
    pScb = [p_.b for p_ in pScT]
    pp = [k.ps(pfx + "pp%d" % i, [128, 512]) for i in range(NB)]
    pO = pp
    pj = pp + pScT

    for i in range(8):
        k.memset("pool", vr[i][:], 1.0, (vr[i].b,))

    xin_v = xin.ap.rearrange("(c p) s -> p c s", p=128)
    xout_v = xout.ap.rearrange("(c p) s -> p c s", p=128)

    def load(t):
        sl = t % 2
        k.dma(xs[sl][:], xin_v[:, :, t * TT:(t + 1) * TT], xch[sl], (xin.b(t),), (xs[sl].b,))

    def pipeline(stages, skew):
        n = len(stages)
        for i in range(n + skew):
            if i < n:
                stages[i][0]()
            if i - skew >= 0:
                stages[i - skew][1]()

    cnt = {"nb": 0, "nj": 0}

    def front_a(t):
        x = xs[t % 2]
        k.act(sq[:], x[:], AF.Square, (x.b,), (sq.b,))
        p0 = pp[cnt["nb"] % NB]
        cnt["nb"] += 1
        for c in range(KC):
            k.mm(p0[:, 0:TT], cst["ones"][:], sq[:, c, :], c == 0, c == KC - 1,
                 (cst["ones"].b, sq.b), (p0.b,))
        k.act(rstd[:], p0[:, 0:TT], AF.Ln, (p0.b,), (rstd.b,), bias=EPS, scale=1.0 / D)
        k.act(rstd[:], rstd[:], AF.Exp, (rstd.b,), (rstd.b,), scale=-0.5)

    def front_h(t, i):
        x = xs[t % 2]
        c = i % KC
        dst, gsrc = (hkv, gkv) if i < KC else (hq, gb)
        k.stt("dve", dst[:, c, :], x[:, c, :], gsrc[:, c:c + 1], rstd[:], ALU.mult, ALU.mult,
              (x.b, gsrc.b, rstd.b), (dst.b,))

    def kq(t):
        kTt = kTr[t % 4]
        stages = []
        for which in range(2):
            for kc in range(KC):
                def mk(which=which, kc=kc, i=cnt["nj"]):
                    wsrc, hsrc, gsrc, dst = (wkv, hkv, kng, kTt) if which == 0 else (wq, hq, qng, qT)
                    p, sqb, rkb = pj[i % NBJ], sqk[i % NBJ], rk[i % NBJ]

                    def first():
                        for c in range(KC):
                            k.mm(p[:, 0:TT], wsrc[:, c, kc * 128:(kc + 1) * 128], hsrc[:, c, :], c == 0, c == KC - 1,
                                 (wsrc.tok(kc * 128), hsrc.b), (p.b,))
                        k.act(sqb[:], p[:, 0:TT], AF.Square, (p.b,), (sqb.b,))

                    def second():
                        k.mm(p[:, TT:2 * TT], bd[:], sqb[:], True, True, (bd.b, sqb.b), (p.b,))
                        k.act(rkb[:], p[:, TT:2 * TT], AF.Ln, (p.b,), (rkb.b,), bias=EPS, scale=1.0 / ADH)
                        k.act(rkb[:], rkb[:], AF.Exp, (rkb.b,), (rkb.b,), scale=-0.5)
                        k.stt("dve", dst[:, kc, :], p[:, 0:TT], gsrc[:, 0:1], rkb[:], ALU.mult, ALU.mult,
                              (p.b, gsrc.b, rkb.b), (dst.b,))

                    return first, second

                stages.append(mk())
                cnt["nj"] += 1
        pipeline(stages, 3)

    def vproj(t):
        for sub in range(NSUB):
            vb = vr[(t * NSUB + sub) % 8]
            for half in range(2):
                p = pj[cnt["nj"] % NBJ]
                cnt["nj"] += 1
                col = D + half * 512
                for c in range(KC):
                    k.mm(p[:, 0:512], hkv[:, c, sub * 128:(sub + 1) * 128], wkv[:, c, col:col + 512], c == 0, c == KC - 1,
                         (wkv.tok(col), hkv.b), (p.b,))
                k.copy("act", vb[:, half * 8:(half + 1) * 8, 0:64], p[:, 0:512].rearrange("p (a b) -> p a b", a=8),
                       (p.b,), (vb.b,))

    def att(t):
        stages = []
        nb = cnt["nb"]
        for sub in range(NSUB):
            for h in range(AH):
                def mk(sub=sub, h=h, i=len(stages), g=nb + (sub * AH + h) // 4):
                    qb = t * NSUB + sub
                    tok = slice(sub * 128, (sub + 1) * 128)
                    j0 = max(0, 4 - qb)
                    onb = on[qb % 2]
                    kc, e = h // 2, h % 2
                    prt = slice(e * 64, (e + 1) * 64)
                    ii = i - (1 if (sub == 1 and t + 1 < NT) else 0)
                    ps, psb, tb, pt = pSc[ii % NS], pScb[ii % NS], tmp[ii % NS], PT[ii % NS]
                    po = pO[g % NB]
                    cs_ = slice(j0 * 128, 640)

                    def first():
                        for j in range(j0, 5):
                            kb = qb - 4 + j
                            kt = kTr[(kb // 2) % 4]
                            ktok = slice((kb % 2) * 128, (kb % 2) * 128 + 128)
                            k.mm(ps[:, j * 128:(j + 1) * 128], kt[prt, kc, ktok], qT[prt, kc, tok], True, True,
                                 (kt.b, qT.b), (psb,))
                        k.stt("dve", tb[:, cs_], ps[:, cs_], 0.125, bias[:, h, cs_], ALU.mult, ALU.add,
                              (psb, bias.b), (tb.b,))
                        k.act(pt[:, cs_], tb[:, cs_], AF.Exp, (tb.b,), (pt.b,))
                        if sub == 1 and t + 1 < NT:
                            front_h(t + 1, h)

                    def second():
                        hl = h % 4
                        for j in range(j0, 5):
                            kb = qb - 4 + j
                            k.mm(po[:, hl * 65:(hl + 1) * 65], pt[:, j * 128:(j + 1) * 128], vr[kb % 8][:, h, :],
                                 j == j0, j == 4, (pt.b, vr[kb % 8].b), (po.b,))
                        if hl == 3:
                            rc = rec[(h // 4) % 2]
                            pv = po[:, 0:260].rearrange("p (a b) -> p a b", a=4)
                            k.recip(rc[:].rearrange("p (a b) -> p a b", b=1), pv[:, :, 64:65], (po.b,), (rc.b,))
                            h4 = h // 4
                            k.tt("dve", onb[:, h4 * 256:(h4 + 1) * 256].rearrange("p (a b) -> p a b", a=4),
                                 pv[:, :, 0:64],
                                 rc[:].rearrange("p (a b) -> p a b", b=1).broadcast_to([128, 4, 64]), ALU.mult,
                                 (po.b, rc.b), (onb.b,))
                        if h == AH - 1:
                            pbk = pp[(g + 1) % NB]
                            pbv = pbk[:].bitcast(BF16)
                            for c in range(KC):
                                k.tr(pbv[:, c * 128:(c + 1) * 128], onb[:, c * 128:(c + 1) * 128], cst["identb"][:],
                                     (onb.b, cst["identb"].b), (pbk.b,))
                            k.copy("act", oT[:, :, tok], pbv.rearrange("p (a b) -> p a b", a=8), (pbk.b,), (oT.b,))

                    return first, second

                stages.append(mk())
            if sub == 0 and t + 1 < NT:
                stages.append((lambda: front_a(t + 1), lambda: None))
        cnt["nb"] = nb + (NSUB * AH) // 4
        pipeline(stages, 2)

    def wo_res(t):
        x = xs[t % 2]
        for oc in range(KC):
            p = pj[cnt["nj"] % NBJ]
            cnt["nj"] += 1
            for c in range(KC):
                k.mm(p[:, 0:TT], wo[:, c, oc * 128:(oc + 1) * 128], oT[:, c, :], c == 0, c == KC - 1,
                     (wo.tok(oc * 128), oT.b), (p.b,))
            k.tt("dve", x[:, oc, :], x[:, oc, :], p[:, 0:TT], ALU.add, (x.b, p.b), (x.b,))
        k.dma(xout_v[:, :, t * TT:(t + 1) * TT], x[:], sch[t % 2], (x.b,), (xout.b(t),))

    load(0)
    if NT > 1:
        load(1)
    front_a(0)
    for i in range(2 * KC):
        front_h(0, i)
    kq(0)
    for t in range(NT):
        vproj(t)
        att(t)
        if t + 1 < NT:
            kq(t + 1)
        wo_res(t)
        if t + 2 < NT:
            load(t + 2)


def _lay(vec, c):
    return np.ascontiguousarray(np.asarray(vec, np.float32).reshape(c, 128).T)


def build_program(S=SEQ):
    nc = bass.Bass("TRN2", target_bir_lowering=False)

    def ext(name, shape, dt=F32):
        return nc.dram_tensor(name, list(shape), dt, kind="ExternalInput").ap()

    x_d = DT(ext("x", [S, D]), "x")
    a_w_in = ext("a_w_in", [D, 6144])
    a_w_o = ext("a_w_o", [2048, D])
    a_w_gu = ext("a_w_gu", [D, 2 * FH])
    a_w_down = ext("a_w_down", [FH, D])
    w_kv = ext("w_kv", [D, 2 * D])
    b_w_q = ext("b_w_q", [D, D])
    b_w_o = ext("b_w_o", [D, D])
    b_w_gu = ext("b_w_gu", [D, 2 * FH])
    b_w_down = ext("b_w_down", [FH, D])
    a_norm_g = ext("a_norm_g", [128, 8])
    a_gn_g = ext("a_gn_g", [128, 16])
    a_ffn_norm_g = ext("a_ffn_norm_g", [128, 8])
    kv_norm_g = ext("kv_norm_g", [128, 8])
    b_norm_g = ext("b_norm_g", [128, 8])
    b_ffn_norm_g = ext("b_ffn_norm_g", [128, 8])
    kng = ext("kng", [128, 1])
    qng = ext("qng", [128, 1])
    bias_d = ext("att_bias", [128, AH, 640])
    bd_d = ext("bd", [128, 128], BF16)
    ones_d = ext("ones", [128, 128], BF16)
    identf_d = ext("identf", [128, 128])
    identb_d = ext("identb", [128, 128], BF16)
    tabs_d = {
        "cs": ext("t_cs", [128, 2 * S]),
        "maskT": ext("t_maskT", [128, RH * 128]), "qdec": ext("t_qdec", [128, RH, 256]),
        "kdec": ext("t_kdec", [128, RH]), "decI": ext("t_decI", [128, RH, 128], BF16),
    }
    out = DT(nc.dram_tensor("out", [S, D], F32, kind="ExternalOutput").ap(), "out")
    xT0 = DT(nc.dram_tensor("xT0", [D, S], F32).ap(), "xT0")
    xT1 = DT(nc.dram_tensor("xT1", [D, S], F32).ap(), "xT1")
    xT2 = DT(nc.dram_tensor("xT2", [D, S], F32).ap(), "xT2")
    xT3 = DT(nc.dram_tensor("xT3", [D, S], F32).ap(), "xT3")
    yT = DT(nc.dram_tensor("yT", [2048, S], BF16).ap(), "yT")

    k = K(nc)
    TT = 256
    k.begin()
    cst = load_consts(k, ones_d, identf_d, identb_d)
    phase_ret(k, cst, S, x_d, xT0, yT, a_w_in, a_norm_g, a_gn_g, tabs_d, "r_")
    k.end()
    k.begin()
    cst = load_consts(k, ones_d, identf_d, identb_d)
    carried = {}

    def prefetch_ffn0():
        carried["w"] = ffn_weights(k, "fa_", a_w_gu, a_w_down, carry=True, only="gu")

    phase_proj_res(k, cst, S, 512, yT, xT0, xT1, a_w_o, 16, "o_", prefetch=prefetch_ffn0)
    k.end()
    k.begin()
    cst = load_consts(k, ones_d, identf_d, identb_d)
    def prefetch_att():
        carried["qo"] = (W(k, "a_wq", b_w_q, 8, D, [(0, D)], carry=True), W(k, "a_wo", b_w_o, 8, D, [(0, D)], carry=True))

    phase_ffn(k, cst, S, TT, xT1, xT2, a_w_gu, a_w_down, a_ffn_norm_g, "fa_", weights=carried["w"],
              prefetch=prefetch_att)
    k.end()
    k.release(carried["w"].t)
    k.begin()
    cst = load_consts(k, ones_d, identf_d, identb_d)
    phase_att(k, cst, S, xT2, xT3, w_kv, b_w_q, b_w_o, kv_norm_g, b_norm_g, kng, qng, bias_d, bd_d, "a_",
              weights=carried["qo"])
    k.end()
    k.release(carried["qo"][0].t)
    k.release(carried["qo"][1].t)
    k.begin()
    cst = load_consts(k, ones_d, identf_d, identb_d)
    phase_ffn(k, cst, S, TT, xT3, None, b_w_gu, b_w_down, b_ffn_norm_g, "fb_", out_tm=out)
    k.end()
    return nc


def kernel(x, a_norm_g, a_w_in, a_gn_g, a_w_o, a_ffn_norm_g, a_w_gu, a_w_down,
           kv_norm_g, w_kv, k_norm_g,
           b_norm_g, b_w_q, b_q_norm_g, b_rel_bias, b_w_o, b_ffn_norm_g, b_w_gu, b_w_down):
    f = lambda a: np.ascontiguousarray(np.asarray(a, dtype=np.float32))
    x = f(x)
    B, S, _ = x.shape
    nc = build_program(S)
    tabs = ret_tables(S)
    shared = {
        "a_w_in": f(a_w_in)[0], "a_w_o": f(a_w_o)[0], "a_w_gu": f(a_w_gu)[0], "a_w_down": f(a_w_down)[0],
        "w_kv": f(w_kv), "b_w_q": f(b_w_q)[0], "b_w_o": f(b_w_o)[0], "b_w_gu": f(b_w_gu)[0],
        "b_w_down": f(b_w_down)[0],
        "a_norm_g": _lay(f(a_norm_g)[0], 8), "a_gn_g": _lay(f(a_gn_g)[0], 16),
        "a_ffn_norm_g": _lay(f(a_ffn_norm_g)[0], 8), "kv_norm_g": _lay(f(kv_norm_g), 8),
        "b_norm_g": _lay(f(b_norm_g)[0], 8), "b_ffn_norm_g": _lay(f(b_ffn_norm_g)[0], 8),
        "kng": np.ascontiguousarray(np.tile(f(k_norm_g), 2)[:, None]),
        "qng": np.ascontiguousarray(np.tile(f(b_q_norm_g)[0], 2)[:, None]),
        "att_bias": att_bias_table(f(b_rel_bias)[0]).reshape(128, AH, 640),
        "bd": np.kron(np.eye(2), np.ones((64, 64))).astype(ml_dtypes.bfloat16),
        "ones": np.ones((128, 128), ml_dtypes.bfloat16),
        "identf": np.eye(128, dtype=np.float32),
        "identb": np.eye(128).astype(ml_dtypes.bfloat16),
        "t_cs": np.ascontiguousarray(
            np.stack([tabs["cos"].reshape(128, S // 256, 256), tabs["sin"].reshape(128, S // 256, 256)], axis=2)
        ).reshape(128, 2 * S),
        "t_maskT": np.ascontiguousarray(tabs["maskT"].reshape(128, RH * 128)),
        "t_qdec": tabs["qdec"], "t_kdec": tabs["kdec"], "t_decI": tabs["decI"],
    }
    in_maps = []
    for b in range(B):
        m = dict(shared)
        m["x"] = x[b]
        in_maps.append(m)
    res = run_bass_kernel_spmd(nc, in_maps, core_ids=list(range(B)))
    return np.stack([np.asarray(r["out"], dtype=np.float32) for r in res.results], axis=0)
```

```python
import math
from contextlib import ExitStack
import numpy as np
import ml_dtypes
import concourse.bass as bass
import concourse.mybir as mybir
from concourse.bass_utils import run_bass_kernel_spmd

F32 = mybir.dt.float32
BF16 = mybir.dt.bfloat16
ALU = mybir.AluOpType
AF = mybir.ActivationFunctionType
AX = mybir.AxisListType

D = 1024
SEQ = 4096
NCORES = 8
EPS = 1e-6
FH = 2816
RH, RDK, RDV = 4, 256, 512
AH, ADH = 16, 64


class Buf:
    __slots__ = ("name", "writers", "readers")

    def __init__(self, name):
        self.name = name
        self.writers = {}
        self.readers = {}


class Chan:
    def __init__(self, name):
        self.name = name
        self.sem = None
        self.sw = False
        self.slot = None
        self.count = 0
        self.last = None


class Op:
    __slots__ = ("eng", "fn", "deps", "sig", "val", "chan", "idx", "key")

    def __init__(self, eng, fn, chan=None):
        self.eng = eng
        self.fn = fn
        self.deps = []
        self.sig = chan is not None
        self.val = None
        self.chan = chan
        self.key = chan if chan is not None else eng


COMPUTE = ("pe", "act", "dve", "pool")
QUEUES = ("pe", "act", "dve", "pool", "sp")


class Sched:
    def __init__(self, nc):
        self.nc = nc
        self.sems = {q: nc.alloc_semaphore(name="sem_" + q) for q in COMPUTE}
        self.semcount = {q: 0 for q in COMPUTE}
        self.dpool = []
        self.swpool = []
        self.nops = 0
        self.last_sw = None
        self.begin()

    def begin(self):
        self.ops = {q: [] for q in QUEUES}
        self.chans = []

    def chan(self, name, sw=False):
        c = Chan(name)
        pool = self.swpool if sw else self.dpool
        i = sum(1 for o in self.chans if o.sw == sw)
        if i >= len(pool):
            pool.append([self.nc.alloc_semaphore(name=("swsem%d" if sw else "dsem%d") % i), 0])
        c.sw = sw
        c.slot = pool[i]
        c.sem = c.slot[0]
        c.count = c.slot[1]
        self.chans.append(c)
        return c

    def _add(self, op, reads, writes):
        eng = op.eng
        key = op.key
        is_dma = op.chan is not None
        deps = {}

        def dep(o):
            k = o.key
            if k not in deps or deps[k].idx < o.idx:
                deps[k] = o

        for b in reads:
            for k, w in b.writers.items():
                if k == key and eng == "pe" and not is_dma:
                    continue
                dep(w)
        for b in writes:
            for k, w in b.writers.items():
                if k == key and not is_dma:
                    continue
                dep(w)
            for k, r in b.readers.items():
                if k == key and not is_dma:
                    continue
                dep(r)
        if is_dma and op.chan.last is not None:
            dep(op.chan.last)
        if is_dma and op.chan.sw:
            if self.last_sw is not None:
                dep(self.last_sw)
            self.last_sw = op
        op.idx = self.nops
        self.nops += 1
        op.deps = list(deps.values())
        for o in op.deps:
            o.sig = True
        for b in reads:
            b.readers[key] = op
        for b in writes:
            b.writers = {key: op}
            b.readers = {}
        if is_dma:
            op.chan.last = op
        self.ops[eng].append(op)
        return op

    def op(self, eng, fn, reads=(), writes=()):
        return self._add(Op(eng, fn), reads, writes)

    def dma(self, queue, out, in_, chan, reads=(), writes=()):
        def fn(e):
            return e.dma_start(out=out, in_=in_)

        return self._add(Op(queue, fn, chan), reads, writes)

    def emit(self):
        nc = self.nc
        sems = self.sems
        for q in QUEUES:
            for o in self.ops[q]:
                if o.chan is not None:
                    o.chan.count += 16
                    o.val = o.chan.count
                elif o.sig:
                    self.semcount[q] += 1
                    o.val = self.semcount[q]
        for c in self.chans:
            c.slot[1] = c.count
        engines = {"pe": "tensor", "act": "scalar", "dve": "vector", "pool": "gpsimd", "sp": "sync"}
        stats = {}

        def run_queue(q):
            def body(e):
                waited = {}
                nw = 0
                for o in self.ops[q]:
                    for d in o.deps:
                        s = d.chan.sem if d.chan is not None else sems[d.eng]
                        if waited.get(s, 0) >= d.val:
                            continue
                        e.wait_ge(s, d.val)
                        nw += 1
                        waited[s] = d.val
                    ins = o.fn(e)
                    if o.chan is not None:
                        ins.then_inc(o.chan.sem, 16)
                    elif o.sig:
                        ins.then_inc(sems[q], 1)
                if q == "sp":
                    for c in self.chans:
                        if c.last is not None and waited.get(c.sem, 0) < c.count:
                            e.wait_ge(c.sem, c.count)
                    for cq in COMPUTE:
                        if self.semcount[cq] > 0:
                            e.wait_ge(sems[cq], self.semcount[cq])
                stats[q] = (len(self.ops[q]), nw)

            return body

        with nc.Block() as block:
            for q in QUEUES:
                getattr(block, engines[q])(run_queue(q))
        self.stats = stats
        self.begin()


class T:
    def __init__(self, k, name, shape, dtype, psum=False, carry=False):
        nc = k.nc
        self.k = k
        if psum:
            self.h = k.es.enter_context(nc.psum_tensor("p%d_%s" % (k.phase, name), shape, dtype))
        else:
            n = 1
            for d in shape[1:]:
                n *= d
            self.nbytes = n * (2 if dtype == BF16 else 4)
            self.off = k._alloc(self.nbytes, top=False)
            self.h = nc.alloc_sbuf_tensor_at("s%d_%s" % (k.phase, name), shape, dtype, offset=self.off)
            if not carry:
                k.local.append(self)
        self.b = Buf(name)
        self.shape = shape

    def __getitem__(self, idx):
        return self.h[idx]


SB_LO, SB_HI = 16512, 229376


class K:
    def __init__(self, nc):
        self.nc = nc
        self.s = Sched(nc)
        self.es = None
        self.phase = 0
        self.free = [[SB_LO, SB_HI]]
        self.local = []

    def _alloc(self, n, top=False):
        n = (n + 63) // 64 * 64
        blocks = reversed(self.free) if top else self.free
        for blk in blocks:
            if blk[1] - blk[0] >= n:
                if top:
                    blk[1] -= n
                    off = blk[1]
                else:
                    off = blk[0]
                    blk[0] += n
                self.free = [b for b in self.free if b[1] > b[0]]
                return off
        raise MemoryError("SBUF exhausted: need %d, free %s" % (n, self.free))

    def _release(self, off, n):
        n = (n + 63) // 64 * 64
        self.free.append([off, off + n])
        self.free.sort()
        merged = []
        for b in self.free:
            if merged and merged[-1][1] == b[0]:
                merged[-1][1] = b[1]
            else:
                merged.append(list(b))
        self.free = merged

    def release(self, t):
        self._release(t.off, t.nbytes)

    def begin(self):
        self.es = ExitStack()
        self.es.__enter__()
        self.phase += 1

    def end(self):
        for q in ("act", "dve", "pool"):
            if not self.s.ops[q]:
                d = self.sb("dummy_" + q, [128, 16], F32)
                self.memset(q if q != "act" else "dve", d[:], 0.0, (d.b,))
        self.s.emit()
        self.es.__exit__(None, None, None)
        self.es = None
        for t in self.local:
            self._release(t.off, t.nbytes)
        self.local = []

    def sb(self, name, shape, dtype, carry=False):
        return T(self, name, shape, dtype, carry=carry)

    def ps(self, name, shape, dtype=F32):
        return T(self, name, shape, dtype, psum=True)

    def mm(self, out, lhsT, rhs, start, stop, reads, writes):
        self.s.op("pe", lambda e: e.matmul(out, lhsT, rhs, start=start, stop=stop), reads, writes)

    def tr(self, out, in_, ident, reads, writes):
        self.s.op("pe", lambda e: e.transpose(out, in_, ident), reads, writes)

    def act(self, out, in_, func, reads, writes, bias=None, scale=None, accum_out=None):
        kw = {}
        if bias is not None:
            kw["bias"] = bias
        if scale is not None:
            kw["scale"] = scale
        if accum_out is not None:
            kw["accum_out"] = accum_out
        self.s.op("act", lambda e: e.activation(out, in_, func, **kw), reads, writes)

    def tt(self, eng, out, in0, in1, op, reads, writes):
        self.s.op(eng, lambda e: e.tensor_tensor(out, in0, in1, op), reads, writes)

    def ts(self, eng, out, in0, s1, s2, op0, op1, reads, writes):
        if op1 is None:
            self.s.op(eng, lambda e: e.tensor_scalar(out, in0, s1, None, op0), reads, writes)
        else:
            self.s.op(eng, lambda e: e.tensor_scalar(out, in0, s1, s2, op0, op1), reads, writes)

    def stt(self, eng, out, in0, scalar, in1, op0, op1, reads, writes):
        self.s.op(eng, lambda e: e.scalar_tensor_tensor(out, in0, scalar, in1, op0, op1), reads, writes)

    def recip(self, out, in_, reads, writes):
        self.s.op("dve", lambda e: e.reciprocal(out, in_), reads, writes)

    def copy(self, eng, out, in_, reads, writes):
        if eng == "act":
            self.s.op("act", lambda e: e.copy(out, in_), reads, writes)
        elif eng == "dve":
            self.s.op(eng, lambda e: e.tensor_scalar(out, in_, 1.0, None, ALU.mult), reads, writes)
        else:
            self.s.op(eng, lambda e: e.tensor_copy(out, in_), reads, writes)

    def memset(self, eng, ap, val, writes):
        self.s.op(eng, lambda e: e.memset(ap, val), (), writes)

    def dma(self, out, in_, chan, reads, writes, queue="sp"):
        self.s.dma(queue, out, in_, chan, reads, writes)


class DT:
    def __init__(self, ap, name):
        self.ap = ap
        self.name = name
        self.bufs = {}

    def b(self, i):
        if i not in self.bufs:
            self.bufs[i] = Buf("%s_%d" % (self.name, i))
        return self.bufs[i]


def load_consts(k, ones_d, identf_d, identb_d):
    c = {}
    c["ones"] = k.sb("ones", [128, 128], BF16)
    c["identf"] = k.sb("identf", [128, 128], F32)
    c["identb"] = k.sb("identb", [128, 128], BF16)
    ch = k.s.chan("const")
    k.dma(c["ones"][:], ones_d, ch, (), (c["ones"].b,))
    k.dma(c["identf"][:], identf_d, ch, (), (c["identf"].b,))
    k.dma(c["identb"][:], identb_d, ch, (), (c["identb"].b,))
    return c


class W:
    def __init__(self, k, name, w_d, kchunks, ncols, groups, carry=False, nch=1):
        self.t = k.sb(name, [128, kchunks, ncols], BF16, carry=carry)
        self.groups = []
        chans = [k.s.chan("%s_%d" % (name, i), sw=True) for i in range(nch)]
        wv = w_d.rearrange("(c p) n -> p c n", p=128)
        for i, (c0, cn) in enumerate(groups):
            b = Buf("%s_g%d" % (name, i))
            self.groups.append((c0, c0 + cn, b))
            k.dma(self.t[:, :, c0:c0 + cn], wv[:, :, c0:c0 + cn], chans[i % nch], (), (b,), queue="pool")

    def __getitem__(self, idx):
        return self.t[idx]

    def tok(self, col):
        for c0, c1, b in self.groups:
            if c0 <= col < c1:
                return b
        raise KeyError(col)


class WStaged:
    def __init__(self, k, name, w_d, kchunks, ncols):
        self.t = k.sb(name, [128, kchunks, ncols], BF16)
        stg = [k.sb("%s_stg%d" % (name, i), [128, ncols], F32) for i in range(2)]
        chs = [k.s.chan("%s_stg%d" % (name, i)) for i in range(2)]
        self.b = Buf(name + "_all")
        self.toks = []
        for c in range(kchunks):
            s_ = stg[c % 2]
            b = Buf("%s_c%d" % (name, c))
            self.toks.append(b)
            k.dma(s_[:], w_d[c * 128:(c + 1) * 128, :], chs[c % 2], (), (s_.b,))
            k.s.op("pool", lambda e, o=self.t[:, c, :], i=s_[:]: e.tensor_copy(o, i), (s_.b,), (b,))

    def __getitem__(self, idx):
        return self.t[idx]

    def tok(self, col):
        return self.toks[-1]


def ffn_weights(k, pfx, w_gu_d, w_down_d, carry=False, hw_down=False, only=None):
    KC, HC = D // 128, FH // 128
    gu_groups = []
    c0 = 0
    for n in (256, 512, 1024, 1024):
        gu_groups.append((c0, n))
        gu_groups.append((FH + c0, n))
        c0 += n
    if only == "gu":
        return W(k, pfx + "wgu", w_gu_d, KC, 2 * FH, gu_groups, carry=carry)
    if only == "down":
        return W(k, pfx + "wd", w_down_d, HC, D, [(0, 512), (512, 512)], carry=carry)
    wgu = W(k, pfx + "wgu", w_gu_d, KC, 2 * FH, gu_groups, carry=carry)
    if hw_down:
        wd = WStaged(k, pfx + "wd", w_down_d, HC, D)
    else:
        wd = W(k, pfx + "wd", w_down_d, HC, D, [(0, 512), (512, 512)], carry=carry)
    return wgu, wd


def phase_ffn(k, cst, S, TT, xin, xout, w_gu_d, w_down_d, g_d, pfx, out_tm=None, weights=None, prefetch=None,
              hw_down=False):
    nc = k.nc
    NT = S // TT
    KC = D // 128
    HC = FH // 128
    if weights is None:
        wgu, wd = ffn_weights(k, pfx, w_gu_d, w_down_d, hw_down=hw_down)
    elif isinstance(weights, tuple):
        wgu, wd = weights
    else:
        wgu = weights
        wd = ffn_weights(k, pfx, w_gu_d, w_down_d, only="down")
    if prefetch is not None:
        prefetch()
    g = k.sb(pfx + "g", [128, KC], F32)
    chg = k.s.chan(pfx + "g")
    k.dma(g[:], g_d, chg, (), (g.b,))

    xs = [k.sb(pfx + "x%d" % i, [128, KC, TT], F32) for i in range(2)]
    xch = [k.s.chan(pfx + "xl%d" % i) for i in range(2)]
    sch = [k.s.chan(pfx + "xs%d" % i) for i in range(2)]
    sq = k.sb(pfx + "sq", [128, KC, TT], BF16)
    h2 = [k.sb(pfx + "h%d" % i, [128, KC, TT], BF16) for i in range(2)]
    a = [k.sb(pfx + "a%d" % i, [128, TT], BF16) for i in range(HC)]
    sg = [k.sb(pfx + "sg%d" % i, [128, TT], F32) for i in range(2)]
    rstd = k.sb(pfx + "rstd", [128, TT], F32)
    ps_ss = k.ps(pfx + "ps_ss", [128, 512])
    ps_g = [k.ps(pfx + "ps_g%d" % i, [128, TT]) for i in range(2)]
    ps_u = [k.ps(pfx + "ps_u%d" % i, [128, TT]) for i in range(2)]
    ps_o = [k.ps(pfx + "ps_o%d" % i, [128, TT]) for i in range(2)]
    if out_tm is not None:
        NSUB = TT // 128
        otm = [k.sb(pfx + "otm%d" % i, [128, D], F32) for i in range(2)]
        och = [k.s.chan(pfx + "otm%d" % i) for i in range(2)]
        ps_t = k.ps(pfx + "ps_t", [128, 512])

    xin_v = xin.ap.rearrange("(c p) s -> p c s", p=128)
    if out_tm is None:
        xout_v = xout.ap.rearrange("(c p) s -> p c s", p=128)

    def load(t):
        sl = t % 2
        k.dma(xs[sl][:], xin_v[:, :, t * TT:(t + 1) * TT], xch[sl], (xin.b(t),), (xs[sl].b,))

    def front_a(t):
        x = xs[t % 2]
        k.act(sq[:], x[:], AF.Square, (x.b,), (sq.b,))
        for c in range(KC):
            k.mm(ps_ss[:, 0:TT], cst["ones"][:], sq[:, c, :], c == 0, c == KC - 1,
                 (cst["ones"].b, sq.b), (ps_ss.b,))
        k.act(rstd[:], ps_ss[:, 0:TT], AF.Ln, (ps_ss.b,), (rstd.b,), bias=EPS, scale=1.0 / D)
        k.act(rstd[:], rstd[:], AF.Exp, (rstd.b,), (rstd.b,), scale=-0.5)

    def front_h(t, c):
        x, hh = xs[t % 2], h2[t % 2]
        k.stt("dve", hh[:, c, :], x[:, c, :], g[:, c:c + 1], rstd[:], ALU.mult, ALU.mult,
              (x.b, g.b, rstd.b), (hh.b,))

    load(0)
    front_a(0)
    for c in range(KC):
        front_h(0, c)
    ntr = 0
    for t in range(NT):
        x = xs[t % 2]
        h = h2[t % 2]
        if t + 1 < NT:
            load(t + 1)
        for hc in range(HC):
            pg, pu, sgb = ps_g[hc % 2], ps_u[hc % 2], sg[hc % 2]
            for c in range(KC):
                k.mm(pg[:], wgu[:, c, hc * 128:(hc + 1) * 128], h[:, c, :], c == 0, c == KC - 1,
                     (wgu.tok(hc * 128), h.b), (pg.b,))
            for c in range(KC):
                k.mm(pu[:], wgu[:, c, FH + hc * 128:FH + (hc + 1) * 128], h[:, c, :], c == 0, c == KC - 1,
                     (wgu.tok(FH + hc * 128), h.b), (pu.b,))
            k.act(sgb[:], pg[:], AF.Silu, (pg.b,), (sgb.b,))
            k.tt("dve", a[hc][:], sgb[:], pu[:], ALU.mult, (sgb.b, pu.b), (a[hc].b,))
            if t + 1 < NT:
                if hc == 6:
                    front_a(t + 1)
                if 9 <= hc < 9 + KC:
                    front_h(t + 1, hc - 9)
        for oc in range(KC):
            po = ps_o[oc % 2]
            for hc in range(HC):
                wtok = wd.toks[hc] if isinstance(wd, WStaged) else wd.tok(oc * 128)
                k.mm(po[:], wd[:, hc, oc * 128:(oc + 1) * 128], a[hc][:], hc == 0, hc == HC - 1,
                     (wtok, a[hc].b), (po.b,))
            k.tt("dve", x[:, oc, :], x[:, oc, :], po[:], ALU.add, (x.b, po.b), (x.b,))
        if out_tm is None:
            k.dma(xout_v[:, :, t * TT:(t + 1) * TT], x[:], sch[t % 2], (x.b,), (xout.b(t),))
        else:
            for sub in range(NSUB):
                o = otm[ntr % 2]
                for half in range(2):
                    pt2 = ps_t if half == 0 else ps_ss
                    for j in range(4):
                        oc = half * 4 + j
                        k.tr(pt2[:, j * 128:(j + 1) * 128], x[:, oc, sub * 128:(sub + 1) * 128],
                             cst["identf"][:], (x.b, cst["identf"].b), (pt2.b,))
                    k.copy("act", o[:, half * 512:(half + 1) * 512], pt2[:], (pt2.b,), (o.b,))
                r0 = t * TT + sub * 128
                k.dma(out_tm.ap[r0:r0 + 128, :], o[:], och[ntr % 2], (o.b,), (out_tm.b(t),))
                ntr += 1


def phase_proj_res(k, cst, S, TT, yin, xin, xout, w_d, kchunks, pfx, prefetch=None):
    NT = S // TT
    KC = D // 128
    w = W(k, pfx + "w", w_d, kchunks, D, [(0, 256), (256, 768)])
    if prefetch is not None:
        prefetch()
    xs = [k.sb(pfx + "x%d" % i, [128, KC, TT], F32) for i in range(2)]
    ys = [k.sb(pfx + "y%d" % i, [128, kchunks, TT], BF16) for i in range(2)]
    xch = [k.s.chan(pfx + "xl%d" % i) for i in range(2)]
    ych = [k.s.chan(pfx + "yl%d" % i) for i in range(2)]
    sch = [k.s.chan(pfx + "xs%d" % i) for i in range(2)]
    ps_o = [k.ps(pfx + "ps_o%d" % i, [128, TT]) for i in range(2)]
    xin_v = xin.ap.rearrange("(c p) s -> p c s", p=128)
    yin_v = yin.ap.rearrange("(c p) s -> p c s", p=128)
    xout_v = xout.ap.rearrange("(c p) s -> p c s", p=128)

    def load(t):
        sl = t % 2
        k.dma(xs[sl][:], xin_v[:, :, t * TT:(t + 1) * TT], xch[sl], (xin.b(t),), (xs[sl].b,))
        k.dma(ys[sl][:], yin_v[:, :, t * TT:(t + 1) * TT], ych[sl], (yin.b(t),), (ys[sl].b,))

    load(0)
    for t in range(NT):
        x, y = xs[t % 2], ys[t % 2]
        if t + 1 < NT:
            load(t + 1)
        for oc in range(KC):
            po = ps_o[oc % 2]
            for c in range(kchunks):
                k.mm(po[:], w[:, c, oc * 128:(oc + 1) * 128], y[:, c, :], c == 0, c == kchunks - 1,
                     (w.tok(oc * 128), y.b), (po.b,))
            k.tt("dve", x[:, oc, :], x[:, oc, :], po[:], ALU.add, (x.b, po.b), (x.b,))
        k.dma(xout_v[:, :, t * TT:(t + 1) * TT], x[:], sch[t % 2], (x.b,), (xout.b(t),))


def ret_tables(S):
    half = RDK // 2
    inv_freq = (np.float32(10000.0) ** (-(np.arange(half, dtype=np.float32)) / np.float32(half))).astype(np.float32)
    ang = (np.arange(S, dtype=np.float32)[None, :] * inv_freq[:, None]).astype(np.float32)
    tabs = {"cos": np.cos(ang).astype(np.float32), "sin": np.sin(ang).astype(np.float32)}
    gam = 1.0 - 2.0 ** (-5.0 - np.arange(RH, dtype=np.float64))
    t = np.arange(128)
    tt, ss = np.meshgrid(t, t, indexing="ij")
    same = (tt // 64) == (ss // 64)
    lower = (tt // 64) > (ss // 64)
    maskT = np.zeros((128, RH, 128), np.float32)
    qdec = np.zeros((128, RH, 256), np.float32)
    kdec = np.zeros((128, RH), np.float32)
    decI = np.zeros((128, RH, 128), np.float32)
    for h in range(RH):
        g = gam[h]
        M = np.where(same, g ** np.abs(tt - ss), np.where(lower, g ** (tt - ss).clip(0), 0.0))
        maskT[:, h, :] = (M.T / 16.0)
        qdec[:, h, :] = (g ** ((np.arange(256) % 128) + 1.0))[None, :]
        kdec[:, h] = g ** (127.0 - t) / 16.0
        decI[:, h, :] = np.eye(128) * g ** 128.0
    tabs["maskT"] = maskT
    tabs["qdec"] = qdec
    tabs["kdec"] = kdec
    tabs["decI"] = decI.astype(ml_dtypes.bfloat16)
    return tabs


def phase_ret(k, cst, S, x_d, xT_out, yT_out, w_in_d, g_d, gn_d, tabs_d, pfx):
    TT = 256
    NSUB = 2
    NT = S // TT
    KC = D // 128
    VOFF = 2048
    groups = [(0, 256), (1024, 256), (256, 768), (1280, 768), (2048, 1024), (4096, 1024), (3072, 1024), (5120, 1024)]
    w = W(k, pfx + "win", w_in_d, KC, 6144, groups)
    g = k.sb(pfx + "g", [128, KC], F32)
    gnl = k.sb(pfx + "gnl", [128, 16], F32)
    maskT = k.sb(pfx + "maskT", [128, RH * 128], F32)
    qdec = k.sb(pfx + "qdec", [128, RH, 256], F32)
    kdec = k.sb(pfx + "kdec", [128, RH], F32)
    decI = k.sb(pfx + "decI", [128, RH, 128], BF16)
    cch = k.s.chan(pfx + "c")
    k.dma(g[:], g_d, cch, (), (g.b,))
    k.dma(gnl[:], gn_d, cch, (), (gnl.b,))
    k.dma(maskT[:], tabs_d["maskT"], cch, (), (maskT.b,))
    k.dma(qdec[:], tabs_d["qdec"], cch, (), (qdec.b,))
    k.dma(kdec[:], tabs_d["kdec"], cch, (), (kdec.b,))
    k.dma(decI[:], tabs_d["decI"], cch, (), (decI.b,))

    xtm = k.sb(pfx + "xtm", [128, NSUB, D], F32)
    xch = k.s.chan(pfx + "xl")
    cs = k.sb(pfx + "cs", [128, 2, TT], F32)
    csch = k.s.chan(pfx + "cs")
    xT = k.sb(pfx + "xT", [128, KC, TT], F32)
    xsch = k.s.chan(pfx + "xs")
    sq = k.sb(pfx + "sq", [128, KC, TT], BF16)
    h = k.sb(pfx + "h", [128, KC, TT], BF16)
    rstd = k.sb(pfx + "rstd", [128, TT], F32)
    qT = [k.sb(pfx + "qT%d" % i, [128, 2, TT], BF16) for i in range(RH)]
    kT = [k.sb(pfx + "kT%d" % i, [128, 2, TT], BF16) for i in range(RH)]
    qdT = [k.sb(pfx + "qdT%d" % i, [128, 2, TT], BF16) for i in range(RH)]
    rt = [[k.sb(pfx + "rt%d_%d" % (i, j), [128, TT], F32) for j in range(4)] for i in range(2)]
    v = k.sb(pfx + "v", [128, 2048], BF16)
    sg = k.sb(pfx + "sg", [128, 2048], F32)
    kd = k.sb(pfx + "kd", [128, RH * 256], BF16)
    ST = [k.sb(pfx + "ST%d" % i, [128, RH * 128], BF16) for i in range(2)]
    st = [[k.sb(pfx + "st%d_%d" % (hd, i), [128, 2, 512], BF16) for i in range(2)] for hd in range(RH)]
    junk = k.sb(pfx + "junk", [128, 512], BF16)
    ssq = [k.sb(pfx + "ssq%d" % i, [128, 1], F32) for i in range(4)]
    y = [k.sb(pfx + "y%d" % i, [128, 2048], BF16) for i in range(2)]
    yT = k.sb(pfx + "yT", [128, 16, TT], BF16)
    ysch = k.s.chan(pfx + "ys")

    pq = [k.ps(pfx + "pq%d" % i, [128, 512]) for i in range(4)]
    pX = k.ps(pfx + "pX", [128, 512])
    pS = k.ps(pfx + "pS", [128, 512])
    pSt = k.ps(pfx + "pSt", [128, 512])
    pB = k.ps(pfx + "pB", [128, 1024], BF16)
    pst2 = [pSt, pX]

    for hd in range(RH):
        k.memset("pool", st[hd][0][:], 0.0, (st[hd][0].b,))

    x_v = x_d.ap.rearrange("(n p) d -> p n d", p=128)
    xTo_v = xT_out.ap.rearrange("(c p) s -> p c s", p=128)
    yTo_v = yT_out.ap.rearrange("(c p) s -> p c s", p=128)
    cnt = {"pair": 0, "vg": 0, "su": 0, "blk": 0}

    def load_x(t):
        k.dma(xtm[:], x_v[:, t * NSUB:(t + 1) * NSUB, :], xch, (x_d.b(t),), (xtm.b,))

    def load_cs(t):
        k.dma(cs[:, 0, :], tabs_d["cos"][:, t * TT:(t + 1) * TT], csch, (), (cs.b,))
        k.dma(cs[:, 1, :], tabs_d["sin"][:, t * TT:(t + 1) * TT], csch, (), (cs.b,))

    def front_a(t):
        for c2 in range(KC // 2):
            pt_ = pst2[c2 % 2]
            for j in range(2):
                c = c2 * 2 + j
                for sub in range(NSUB):
                    k.tr(pt_[:, j * 256 + sub * 128: j * 256 + (sub + 1) * 128],
                         xtm[:, sub, c * 128:(c + 1) * 128], cst["identf"][:],
                         (xtm.b, cst["identf"].b), (pt_.b,))
            k.copy("dve", xT[:, c2 * 2:c2 * 2 + 2, :], pt_[:].rearrange("p (a b) -> p a b", a=2),
                   (pt_.b,), (xT.b,))
        k.dma(xTo_v[:, :, t * TT:(t + 1) * TT], xT[:], xsch, (xT.b,), (xT_out.b(t),))
        k.act(sq[:], xT[:], AF.Square, (xT.b,), (sq.b,))

    def front_b(t):
        for c in range(KC):
            k.mm(pS[:, 0:TT], cst["ones"][:], sq[:, c, :], c == 0, c == KC - 1,
                 (cst["ones"].b, sq.b), (pS.b,))
        k.act(rstd[:], pS[:, 0:TT], AF.Ln, (pS.b,), (rstd.b,), bias=EPS, scale=1.0 / D)
        k.act(rstd[:], rstd[:], AF.Exp, (rstd.b,), (rstd.b,), scale=-0.5)
        for c in range(KC):
            k.stt("dve", h[:, c, :], xT[:, c, :], g[:, c:c + 1], rstd[:], ALU.mult, ALU.mult,
                  (xT.b, g.b, rstd.b), (h.b,))

    def qk(t):
        cos, sin = cs[:, 0, :], cs[:, 1, :]
        for hd in range(RH):
            for which in range(2):
                dst = qT[hd] if which == 0 else kT[hd]
                i = cnt["pair"] % 2
                cnt["pair"] += 1
                pa, pb = pq[2 * i], pq[2 * i + 1]
                r = rt[i]
                for j, p in enumerate((pa, pb)):
                    col = (which * 8 + hd * 2 + j) * 128
                    for c in range(KC):
                        k.mm(p[:, 0:TT], w[:, c, col:col + 128], h[:, c, :], c == 0, c == KC - 1,
                             (w.tok(col), h.b), (p.b,))
                a_, b_ = pa[:, 0:TT], pb[:, 0:TT]
                k.tt("dve", r[0][:], a_, cos, ALU.mult, (pa.b, cs.b), (r[0].b,))
                k.tt("dve", r[1][:], b_, sin, ALU.mult, (pb.b, cs.b), (r[1].b,))
                k.tt("dve", r[2][:], a_, sin, ALU.mult, (pa.b, cs.b), (r[2].b,))
                k.tt("dve", r[3][:], b_, cos, ALU.mult, (pb.b, cs.b), (r[3].b,))
                k.tt("pool", dst[:, 0, :], r[0][:], r[1][:], ALU.subtract, (r[0].b, r[1].b), (dst.b,))
                k.tt("pool", dst[:, 1, :], r[2][:], r[3][:], ALU.add, (r[2].b, r[3].b), (dst.b,))
                if which == 0:
                    for j in range(2):
                        k.tt("pool", qdT[hd][:, j, :], dst[:, j, :], qdec[:, hd, :], ALU.mult,
                             (dst.b, qdec.b), (qdT[hd].b,))

    def vg(t, sub):
        for cb in range(8):
            p = pq[cnt["vg"] % 2]
            cnt["vg"] += 1
            col = VOFF + cb * 512
            for c in range(KC):
                k.mm(p[:], h[:, c, sub * 128:(sub + 1) * 128], w[:, c, col:col + 512], c == 0, c == KC - 1,
                     (w.tok(col), h.b), (p.b,))
            if cb < 4:
                k.copy("act", v[:, cb * 512:(cb + 1) * 512], p[:], (p.b,), (v.b,))
            else:
                cc = cb - 4
                k.act(sg[:, cc * 512:(cc + 1) * 512], p[:], AF.Silu, (p.b,), (sg.b,))

    def kt(t, sub):
        for hd in range(RH):
            for j in range(2):
                k.tr(pB[:, hd * 256 + j * 128: hd * 256 + (j + 1) * 128],
                     kT[hd][:, j, sub * 128:(sub + 1) * 128], cst["identb"][:],
                     (kT[hd].b, cst["identb"].b), (pB.b,))
        for hd in range(RH):
            k.act(kd[:, hd * 256:(hd + 1) * 256], pB[:, hd * 256:(hd + 1) * 256], AF.Copy,
                  (pB.b, kdec.b), (kd.b,), scale=kdec[:, hd:hd + 1])

    def scores(t, sub):
        tok = slice(sub * 128, (sub + 1) * 128)
        STb = ST[cnt["blk"] % 2]
        for hd in range(RH):
            for j in range(2):
                k.mm(pS[:, hd * 128:(hd + 1) * 128], kT[hd][:, j, tok], qT[hd][:, j, tok], j == 0, j == 1,
                     (kT[hd].b, qT[hd].b), (pS.b,))
        k.tt("dve", STb[:], pS[:], maskT[:], ALU.mult, (pS.b, maskT.b), (STb.b,))

    def core(t, sub):
        tok = slice(sub * 128, (sub + 1) * 128)
        n = cnt["blk"]
        cnt["blk"] += 1
        cur, nxt = n % 2, (n + 1) % 2
        STb = ST[n % 2]
        yb = y[sub]
        for hd in range(RH):
            po = pq[2 + hd % 2]
            vh = v[:, hd * 512:(hd + 1) * 512]
            for j in range(2):
                pst = pst2[cnt["su"] % 2]
                cnt["su"] += 1
                k.mm(pst[:], kd[:, hd * 256 + j * 128: hd * 256 + (j + 1) * 128], vh, True, False,
                     (kd.b, v.b), (pst.b,))
                k.mm(pst[:], decI[:, hd, :], st[hd][cur][:, j, :], False, True,
                     (decI.b, st[hd][cur].b), (pst.b,))
                k.copy("act" if j == 0 else "dve", st[hd][nxt][:, j, :], pst[:], (pst.b,), (st[hd][nxt].b,))
            k.mm(po[:], STb[:, hd * 128:(hd + 1) * 128], vh, True, False, (STb.b, v.b), (po.b,))
            for j in range(2):
                k.mm(po[:], qdT[hd][:, j, tok], st[hd][cur][:, j, :], False, j == 1,
                     (qdT[hd].b, st[hd][cur].b), (po.b,))
            sq_ = ssq[hd]
            k.act(junk[:], po[:], AF.Square, (po.b,), (junk.b, sq_.b), accum_out=sq_[:])
            k.act(sq_[:], sq_[:], AF.Ln, (sq_.b,), (sq_.b,), bias=EPS, scale=1.0 / RDV)
            k.act(sq_[:], sq_[:], AF.Exp, (sq_.b,), (sq_.b,), scale=-0.5)
            k.stt("dve", yb[:, hd * 512:(hd + 1) * 512], po[:], sq_[:, 0:1], sg[:, hd * 512:(hd + 1) * 512],
                  ALU.mult, ALU.mult, (po.b, sq_.b, sg.b), (yb.b,))

    def ytrans(t, sub):
        tok = slice(sub * 128, (sub + 1) * 128)
        yb = y[sub]
        for half in range(2):
            pyt = pq[2 + half]
            pyv = pyt[:].bitcast(BF16)
            for c in range(8):
                cc = half * 8 + c
                k.tr(pyv[:, c * 128:(c + 1) * 128], yb[:, cc * 128:(cc + 1) * 128], cst["identb"][:],
                     (yb.b, cst["identb"].b), (pyt.b,))
            k.tt("dve", yT[:, half * 8:(half + 1) * 8, tok], pyv.rearrange("p (a b) -> p a b", a=8),
                 gnl[:, half * 8:(half + 1) * 8].rearrange("p (a b) -> p a b", b=1).broadcast_to([128, 8, 128]),
                 ALU.mult, (pyt.b, gnl.b), (yT.b,))
        if sub == NSUB - 1:
            k.dma(yTo_v[:, :, t * TT:(t + 1) * TT], yT[:], ysch, (yT.b,), (yT_out.b(t),))

    load_x(0)
    load_cs(0)
    front_a(0)
    if NT > 1:
        load_x(1)
    front_b(0)
    for t in range(NT):
        qk(t)
        if t + 1 < NT:
            load_cs(t + 1)
        kt(t, 0)
        scores(t, 0)
        vg(t, 0)
        if t > 0:
            ytrans(t - 1, 1)
        core(t, 0)
        kt(t, 1)
        scores(t, 1)
        vg(t, 1)
        if t + 1 < NT:
            front_a(t + 1)
            if t + 2 < NT:
                load_x(t + 2)
        ytrans(t, 0)
        if t + 1 < NT:
            front_b(t + 1)
        core(t, 1)
    ytrans(NT - 1, 1)


NEG = -30000.0


def att_bias_table(rel_bias):
    rb = np.asarray(rel_bias, np.float32)
    sl = np.arange(128)[:, None, None]
    j = np.arange(5)[None, :, None]
    tl = np.arange(128)[None, None, :]
    dist = (4 - j) * 128 + tl - sl
    idx = np.clip(dist, -256, 256) + 256
    hi = (tl >= 64).astype(np.int64)
    vis = (j * 128 + sl >= 64 * hi) & ((j - 4) * 128 + sl < 64 * (hi + 1))
    tab = rb[:, idx]
    tab = np.where(vis[None], tab, np.float32(NEG))
    return np.ascontiguousarray(tab.transpose(1, 0, 2, 3)).astype(np.float32)


def phase_att(k, cst, S, xin, xout, w_kv_d, w_q_d, w_o_d, gkv_d, gb_d, kng_d, qng_d, bias_d, bd_d, pfx, weights=None):
    TT = 256
    NSUB = 2
    NT = S // TT
    KC = D // 128
    wkv = W(k, pfx + "wkv", w_kv_d, KC, 2 * D, [(0, 256), (256, 768), (D, 1024)])
    if weights is not None:
        wq, wo = weights
    else:
        wq = W(k, pfx + "wq", w_q_d, KC, D, [(0, D)])
        wo = W(k, pfx + "wo", w_o_d, KC, D, [(0, D)])
    gkv = k.sb(pfx + "gkv", [128, KC], F32)
    gb = k.sb(pfx + "gb", [128, KC], F32)
    kng = k.sb(pfx + "kng", [128, 1], F32)
    qng = k.sb(pfx + "qng", [128, 1], F32)
    bias = k.sb(pfx + "bias", [128, AH, 640], F32)
    bd = k.sb(pfx + "bd", [128, 128], BF16)
    cch = k.s.chan(pfx + "c")
    k.dma(gkv[:], gkv_d, cch, (), (gkv.b,))
    k.dma(gb[:], gb_d, cch, (), (gb.b,))
    k.dma(kng[:], kng_d, cch, (), (kng.b,))
    k.dma(qng[:], qng_d, cch, (), (qng.b,))
    k.dma(bd[:], bd_d, cch, (), (bd.b,))
    bch = k.s.chan(pfx + "bias")
    for hh in range(0, AH, 4):
        k.dma(bias[:, hh:hh + 4, :], bias_d[:, hh:hh + 4, :], bch, (), (bias.b,))

    xs = [k.sb(pfx + "x%d" % i, [128, KC, TT], F32) for i in range(2)]
    xch = [k.s.chan(pfx + "xl%d" % i) for i in range(2)]
    sch = [k.s.chan(pfx + "xs%d" % i) for i in range(2)]
    sq = k.sb(pfx + "sq", [128, KC, TT], BF16)
    hkv = k.sb(pfx + "hkv", [128, KC, TT], BF16)
    hq = k.sb(pfx + "hq", [128, KC, TT], BF16)
    rstd = k.sb(pfx + "rstd", [128, TT], F32)
    kTr = [k.sb(pfx + "kTr%d" % i, [128, KC, TT], BF16) for i in range(4)]
    vr = [k.sb(pfx + "vr%d" % i, [128, AH, 65], BF16) for i in range(8)]
    qT = k.sb(pfx + "qT", [128, KC, TT], BF16)
    NB = 2
    NBJ = 5
    sqk = [k.sb(pfx + "sqk%d" % i, [128, TT], BF16) for i in range(NBJ)]
    rk = [k.sb(pfx + "rk%d" % i, [128, TT], F32) for i in range(NBJ)]
    NS = 3
    tmp = [k.sb(pfx + "tmp%d" % i, [128, 640], F32) for i in range(NS)]
    PT = [k.sb(pfx + "PT%d" % i, [128, 640], BF16) for i in range(NS)]
    rec = [k.sb(pfx + "rec%d" % i, [128, 4], F32) for i in range(2)]
    on = [k.sb(pfx + "on%d" % i, [128, D], BF16) for i in range(2)]
    oT = k.sb(pfx + "oT", [128, KC, TT], BF16)

    pScT = [k.ps(pfx + "pSc%d" % i, [128, 1024]) for i in range(NS)]
    pSc = [p_[:] for p_ in pScT]
    pScb = [p_.b for p_ in pScT]
    pp = [k.ps(pfx + "pp%d" % i, [128, 512]) for i in range(NB)]
    pO = pp
    pj = pp + pScT

    for i in range(8):
        k.memset("pool", vr[i][:], 1.0, (vr[i].b,))

    xin_v = xin.ap.rearrange("(c p) s -> p c s", p=128)
    xout_v = xout.ap.rearrange("(c p) s -> p c s", p=128)

    def load(t):
        sl = t % 2
        k.dma(xs[sl][:], xin_v[:, :, t * TT:(t + 1) * TT], xch[sl], (xin.b(t),), (xs[sl].b,))

    def pipeline(stages, skew):
        n = len(stages)
        for i in range(n + skew):
            if i < n:
                stages[i][0]()
            if i - skew >= 0:
                stages[i - skew][1]()

    cnt = {"nb": 0, "nj": 0}

    def front_a(t):
        x = xs[t % 2]
        k.act(sq[:], x[:], AF.Square, (x.b,), (sq.b,))
        p0 = pp[cnt["nb"] % NB]
        cnt["nb"] += 1
        for c in range(KC):
            k.mm(p0[:, 0:TT], cst["ones"][:], sq[:, c, :], c == 0, c == KC - 1,
                 (cst["ones"].b, sq.b), (p0.b,))
        k.act(rstd[:], p0[:, 0:TT], AF.Ln, (p0.b,), (rstd.b,), bias=EPS, scale=1.0 / D)
        k.act(rstd[:], rstd[:], AF.Exp, (rstd.b,), (rstd.b,), scale=-0.5)

    def front_h(t, i):
        x = xs[t % 2]
        c = i % KC
        dst, gsrc = (hkv, gkv) if i < KC else (hq, gb)
        k.stt("dve", dst[:, c, :], x[:, c, :], gsrc[:, c:c + 1], rstd[:], ALU.mult, ALU.mult,
              (x.b, gsrc.b, rstd.b), (dst.b,))

    def kq(t):
        kTt = kTr[t % 4]
        stages = []
        for which in range(2):
            for kc in range(KC):
                def mk(which=which, kc=kc, i=cnt["nj"]):
                    wsrc, hsrc, gsrc, dst = (wkv, hkv, kng, kTt) if which == 0 else (wq, hq, qng, qT)
                    p, sqb, rkb = pj[i % NBJ], sqk[i % NBJ], rk[i % NBJ]

                    def first():
                        for c in range(KC):
                            k.mm(p[:, 0:TT], wsrc[:, c, kc * 128:(kc + 1) * 128], hsrc[:, c, :], c == 0, c == KC - 1,
                                 (wsrc.tok(kc * 128), hsrc.b), (p.b,))
                        k.act(sqb[:], p[:, 0:TT], AF.Square, (p.b,), (sqb.b,))

                    def second():
                        k.mm(p[:, TT:2 * TT], bd[:], sqb[:], True, True, (bd.b, sqb.b), (p.b,))
                        k.act(rkb[:], p[:, TT:2 * TT], AF.Ln, (p.b,), (rkb.b,), bias=EPS, scale=1.0 / ADH)
                        k.act(rkb[:], rkb[:], AF.Exp, (rkb.b,), (rkb.b,), scale=-0.5)
                        k.stt("dve", dst[:, kc, :], p[:, 0:TT], gsrc[:, 0:1], rkb[:], ALU.mult, ALU.mult,
                              (p.b, gsrc.b, rkb.b), (dst.b,))

                    return first, second

                stages.append(mk())
                cnt["nj"] += 1
        pipeline(stages, 3)

    def vproj(t):
        for sub in range(NSUB):
            vb = vr[(t * NSUB + sub) % 8]
            for half in range(2):
                p = pj[cnt["nj"] % NBJ]
                cnt["nj"] += 1
                col = D + half * 512
                for c in range(KC):
                    k.mm(p[:, 0:512], hkv[:, c, sub * 128:(sub + 1) * 128], wkv[:, c, col:col + 512], c == 0, c == KC - 1,
                         (wkv.tok(col), hkv.b), (p.b,))
                k.copy("act", vb[:, half * 8:(half + 1) * 8, 0:64], p[:, 0:512].rearrange("p (a b) -> p a b", a=8),
                       (p.b,), (vb.b,))

    def att(t):
        stages = []
        nb = cnt["nb"]
        for sub in range(NSUB):
            for h in range(AH):
                def mk(sub=sub, h=h, i=len(stages), g=nb + (sub * AH + h) // 4):
                    qb = t * NSUB + sub
                    tok = slice(sub * 128, (sub + 1) * 128)
                    j0 = max(0, 4 - qb)
                    onb = on[qb % 2]
                    kc, e = h // 2, h % 2
                    prt = slice(e * 64, (e + 1) * 64)
                    ii = i - (1 if (sub == 1 and t + 1 < NT) else 0)
                    ps, psb, tb, pt = pSc[ii % NS], pScb[ii % NS], tmp[ii % NS], PT[ii % NS]
                    po = pO[g % NB]
                    cs_ = slice(j0 * 128, 640)

                    def first():
                        for j in range(j0, 5):
                            kb = qb - 4 + j
                            kt = kTr[(kb // 2) % 4]
                            ktok = slice((kb % 2) * 128, (kb % 2) * 128 + 128)
                            k.mm(ps[:, j * 128:(j + 1) * 128], kt[prt, kc, ktok], qT[prt, kc, tok], True, True,
                                 (kt.b, qT.b), (psb,))
                        k.stt("dve", tb[:, cs_], ps[:, cs_], 0.125, bias[:, h, cs_], ALU.mult, ALU.add,
                              (psb, bias.b), (tb.b,))
                        k.act(pt[:, cs_], tb[:, cs_], AF.Exp, (tb.b,), (pt.b,))
                        if sub == 1 and t + 1 < NT:
                            front_h(t + 1, h)

                    def second():
                        hl = h % 4
                        for j in range(j0, 5):
                            kb = qb - 4 + j
                            k.mm(po[:, hl * 65:(hl + 1) * 65], pt[:, j * 128:(j + 1) * 128], vr[kb % 8][:, h, :],
                                 j == j0, j == 4, (pt.b, vr[kb % 8].b), (po.b,))
                        if hl == 3:
                            rc = rec[(h // 4) % 2]
                            pv = po[:, 0:260].rearrange("p (a b) -> p a b", a=4)
                            k.recip(rc[:].rearrange("p (a b) -> p a b", b=1), pv[:, :, 64:65], (po.b,), (rc.b,))
                            h4 = h // 4
                            k.tt("dve", onb[:, h4 * 256:(h4 + 1) * 256].rearrange("p (a b) -> p a b", a=4),
                                 pv[:, :, 0:64],
                                 rc[:].rearrange("p (a b) -> p a b", b=1).broadcast_to([128, 4, 64]), ALU.mult,
                                 (po.b, rc.b), (onb.b,))
                        if h == AH - 1:
                            pbk = pp[(g + 1) % NB]
                            pbv = pbk[:].bitcast(BF16)
                            for c in range(KC):
                                k.tr(pbv[:, c * 128:(c + 1) * 128], onb[:, c * 128:(c + 1) * 128], cst["identb"][:],
                                     (onb.b, cst["identb"].b), (pbk.b,))
                            k.copy("act", oT[:, :, tok], pbv.rearrange("p (a b) -> p a b", a=8), (pbk.b,), (oT.b,))

                    return first, second

                stages.append(mk())
            if sub == 0 and t + 1 < NT:
                stages.append((lambda: front_a(t + 1), lambda: None))
        cnt["nb"] = nb + (NSUB * AH) // 4
        pipeline(stages, 2)

    def wo_res(t):
        x = xs[t % 2]
        for oc in range(KC):
            p = pj[cnt["nj"] % NBJ]
            cnt["nj"] += 1
            for c in range(KC):
                k.mm(p[:, 0:TT], wo[:, c, oc * 128:(oc + 1) * 128], oT[:, c, :], c == 0, c == KC - 1,
                     (wo.tok(oc * 128), oT.b), (p.b,))
            k.tt("dve", x[:, oc, :], x[:, oc, :], p[:, 0:TT], ALU.add, (x.b, p.b), (x.b,))
        k.dma(xout_v[:, :, t * TT:(t + 1) * TT], x[:], sch[t % 2], (x.b,), (xout.b(t),))

    load(0)
    if NT > 1:
        load(1)
    front_a(0)
    for i in range(2 * KC):
        front_h(0, i)
    kq(0)
    for t in range(NT):
        vproj(t)
        att(t)
        if t + 1 < NT:
            kq(t + 1)
        wo_res(t)
        if t + 2 < NT:
            load(t + 2)


def _lay(vec, c):
    return np.ascontiguousarray(np.asarray(vec, np.float32).reshape(c, 128).T)


def build_program(S=SEQ):
    nc = bass.Bass("TRN2", target_bir_lowering=False)

    def ext(name, shape, dt=F32):
        return nc.dram_tensor(name, list(shape), dt, kind="ExternalInput").ap()

    x_d = DT(ext("x", [S, D]), "x")
    a_w_in = ext("a_w_in", [D, 6144])
    a_w_o = ext("a_w_o", [2048, D])
    a_w_gu = ext("a_w_gu", [D, 2 * FH])
    a_w_down = ext("a_w_down", [FH, D])
    w_kv = ext("w_kv", [D, 2 * D])
    b_w_q = ext("b_w_q", [D, D])
    b_w_o = ext("b_w_o", [D, D])
    b_w_gu = ext("b_w_gu", [D, 2 * FH])
    b_w_down = ext("b_w_down", [FH, D])
    a_norm_g = ext("a_norm_g", [128, 8])
    a_gn_g = ext("a_gn_g", [128, 16])
    a_ffn_norm_g = ext("a_ffn_norm_g", [128, 8])
    kv_norm_g = ext("kv_norm_g", [128, 8])
    b_norm_g = ext("b_norm_g", [128, 8])
    b_ffn_norm_g = ext("b_ffn_norm_g", [128, 8])
    kng = ext("kng", [128, 1])
    qng = ext("qng", [128, 1])
    bias_d = ext("att_bias", [128, AH, 640])
    bd_d = ext("bd", [128, 128], BF16)
    ones_d = ext("ones", [128, 128], BF16)
    identf_d = ext("identf", [128, 128])
    identb_d = ext("identb", [128, 128], BF16)
    tabs_d = {
        "cos": ext("t_cos", [128, S]), "sin": ext("t_sin", [128, S]),
        "maskT": ext("t_maskT", [128, RH * 128]), "qdec": ext("t_qdec", [128, RH, 256]),
        "kdec": ext("t_kdec", [128, RH]), "decI": ext("t_decI", [128, RH, 128], BF16),
    }
    out = DT(nc.dram_tensor("out", [S, D], F32, kind="ExternalOutput").ap(), "out")
    xT0 = DT(nc.dram_tensor("xT0", [D, S], F32).ap(), "xT0")
    xT1 = DT(nc.dram_tensor("xT1", [D, S], F32).ap(), "xT1")
    xT2 = DT(nc.dram_tensor("xT2", [D, S], F32).ap(), "xT2")
    xT3 = DT(nc.dram_tensor("xT3", [D, S], F32).ap(), "xT3")
    yT = DT(nc.dram_tensor("yT", [2048, S], BF16).ap(), "yT")

    k = K(nc)
    TT = 256
    k.begin()
    cst = load_consts(k, ones_d, identf_d, identb_d)
    phase_ret(k, cst, S, x_d, xT0, yT, a_w_in, a_norm_g, a_gn_g, tabs_d, "r_")
    k.end()
    k.begin()
    cst = load_consts(k, ones_d, identf_d, identb_d)
    carried = {}

    def prefetch_ffn0():
        carried["w"] = ffn_weights(k, "fa_", a_w_gu, a_w_down, carry=True, only="gu")

    phase_proj_res(k, cst, S, 512, yT, xT0, xT1, a_w_o, 16, "o_", prefetch=prefetch_ffn0)
    k.end()
    k.begin()
    cst = load_consts(k, ones_d, identf_d, identb_d)
    def prefetch_att():
        carried["qo"] = (W(k, "a_wq", b_w_q, 8, D, [(0, D)], carry=True), W(k, "a_wo", b_w_o, 8, D, [(0, D)], carry=True))

    phase_ffn(k, cst, S, TT, xT1, xT2, a_w_gu, a_w_down, a_ffn_norm_g, "fa_", weights=carried["w"],
              prefetch=prefetch_att)
    k.end()
    k.release(carried["w"].t)
    k.begin()
    cst = load_consts(k, ones_d, identf_d, identb_d)
    phase_att(k, cst, S, xT2, xT3, w_kv, b_w_q, b_w_o, kv_norm_g, b_norm_g, kng, qng, bias_d, bd_d, "a_",
              weights=carried["qo"])
    k.end()
    k.release(carried["qo"][0].t)
    k.release(carried["qo"][1].t)
    k.begin()
    cst = load_consts(k, ones_d, identf_d, identb_d)
    phase_ffn(k, cst, S, TT, xT3, None, b_w_gu, b_w_down, b_ffn_norm_g, "fb_", out_tm=out)
    k.end()
    return nc


def kernel(x, a_norm_g, a_w_in, a_gn_g, a_w_o, a_ffn_norm_g, a_w_gu, a_w_down,
           kv_norm_g, w_kv, k_norm_g,
           b_norm_g, b_w_q, b_q_norm_g, b_rel_bias, b_w_o, b_ffn_norm_g, b_w_gu, b_w_down):
    f = lambda a: np.ascontiguousarray(np.asarray(a, dtype=np.float32))
    x = f(x)
    B, S, _ = x.shape
    nc = build_program(S)
    tabs = ret_tables(S)
    shared = {
        "a_w_in": f(a_w_in)[0], "a_w_o": f(a_w_o)[0], "a_w_gu": f(a_w_gu)[0], "a_w_down": f(a_w_down)[0],
        "w_kv": f(w_kv), "b_w_q": f(b_w_q)[0], "b_w_o": f(b_w_o)[0], "b_w_gu": f(b_w_gu)[0],
        "b_w_down": f(b_w_down)[0],
        "a_norm_g": _lay(f(a_norm_g)[0], 8), "a_gn_g": _lay(f(a_gn_g)[0], 16),
        "a_ffn_norm_g": _lay(f(a_ffn_norm_g)[0], 8), "kv_norm_g": _lay(f(kv_norm_g), 8),
        "b_norm_g": _lay(f(b_norm_g)[0], 8), "b_ffn_norm_g": _lay(f(b_ffn_norm_g)[0], 8),
        "kng": np.ascontiguousarray(np.tile(f(k_norm_g), 2)[:, None]),
        "qng": np.ascontiguousarray(np.tile(f(b_q_norm_g)[0], 2)[:, None]),
        "att_bias": att_bias_table(f(b_rel_bias)[0]).reshape(128, AH, 640),
        "bd": np.kron(np.eye(2), np.ones((64, 64))).astype(ml_dtypes.bfloat16),
        "ones": np.ones((128, 128), ml_dtypes.bfloat16),
        "identf": np.eye(128, dtype=np.float32),
        "identb": np.eye(128).astype(ml_dtypes.bfloat16),
        "t_cos": tabs["cos"], "t_sin": tabs["sin"],
        "t_maskT": np.ascontiguousarray(tabs["maskT"].reshape(128, RH * 128)),
        "t_qdec": tabs["qdec"], "t_kdec": tabs["kdec"], "t_decI": tabs["decI"],
    }
    in_maps = []
    for b in range(B):
        m = dict(shared)
        m["x"] = x[b]
        in_maps.append(m)
    res = run_bass_kernel_spmd(nc, in_maps, core_ids=list(range(B)))
    return np.stack([np.asarray(r["out"], dtype=np.float32) for r in res.results], axis=0)
```
